# Optimizing a Trainium2 kernel written in Bass

```python
import jax, jax.numpy as jnp
from jax import lax
import numpy as np

D_MODEL = 2048
BATCH = 2
SEQ = 8192
DEPTH = 1

CHUNK = 64
MEM_LEN = 256
D_MIX = D_MODEL
D_POOL = D_MIX // 2
POOL_WINDOWS = (2, 4, 8, 16)
N_POOL_GROUPS = len(POOL_WINDOWS)
POOL_GROUP = D_POOL // N_POOL_GROUPS
D_GLA = D_MIX - D_POOL
GLA_HEADS = 4
GLA_DV = D_GLA // GLA_HEADS
GLA_DK = GLA_DV // 2
D_GLA_K = GLA_HEADS * GLA_DK
GLA_GATE_RANK = 16
GLA_GATE_TEMP = 16.0
XATTN_HEADS = 4
XATTN_HEAD_DIM = D_MODEL // XATTN_HEADS
D_FF = ((8 * D_MODEL // 3 + 255) // 256) * 256
RMS_EPS = 1e-6
IN_SPLITS = (D_POOL,
             D_POOL + D_GLA_K,
             D_POOL + 2 * D_GLA_K,
             D_POOL + 2 * D_GLA_K + D_GLA,
             D_POOL + 2 * D_GLA_K + 2 * D_GLA)
IN_COLS = D_POOL + 2 * D_GLA_K + 2 * D_GLA + GLA_GATE_RANK

kernel_name = "hymba_pool_gla_macaron_memxattn"


def rms_norm(x, gain):
    xf = x.astype(jnp.float32)
    y = xf * lax.rsqrt(jnp.mean(xf * xf, axis=-1, keepdims=True) + RMS_EPS)
    return (y * gain.astype(jnp.float32)).astype(x.dtype)


def swiglu(h, w_gate, w_up, w_down):
    return (jax.nn.silu(h @ w_gate) * (h @ w_up)) @ w_down


def pool_mixer(u, pool_w, pool_scale):
    b, s, _ = u.shape
    uf = u.astype(jnp.float32).reshape(b, s, N_POOL_GROUPS, POOL_GROUP)
    cs = jnp.cumsum(uf, axis=1)
    count = jnp.arange(1, s + 1, dtype=jnp.float32)
    diffs = []
    for g, w in enumerate(POOL_WINDOWS):
        cs_g = cs[:, :, g]
        lagged = jnp.pad(cs_g, ((0, 0), (w, 0), (0, 0)))[:, :s]
        mean = (cs_g - lagged) / jnp.minimum(count, float(w))[None, :, None]
        diffs.append(mean - uf[:, :, g])
    d = jnp.stack(diffs, axis=2).astype(u.dtype)
    y = jnp.einsum('bsgc,gcd->bsgd', d, pool_w)
    return y.reshape(b, s, D_POOL) * pool_scale


def gla_mixer(q, k, v, g, a_lr, w_a2, b_a, head_norm):
    b, s, _ = q.shape
    nc = s // CHUNK
    dt = v.dtype
    log_a = jax.nn.log_sigmoid((a_lr @ w_a2 + b_a).astype(jnp.float32)) / GLA_GATE_TEMP

    def chunks(t, d):
        return t.astype(jnp.float32).reshape(b, nc, CHUNK, GLA_HEADS, d)

    qc = chunks(q, GLA_DK) * (GLA_DK ** -0.5)
    kc = chunks(k, GLA_DK)
    vc = chunks(v, GLA_DV)
    cum = jnp.cumsum(chunks(log_a, GLA_DK), axis=2)
    b_end = cum[:, :, -1:]
    k_dec = kc * jnp.exp(b_end - cum)
    q_dec = qc * jnp.exp(b_end)
    scores = jnp.einsum('bnihk,bnjhk->bnhij', qc, k_dec)
    o_intra = jnp.einsum('bnhij,bnjhv->bnihv', scores, vc)

    def step(state, xs):
        q_c, k_c, v_c, decay_c = xs
        o = jnp.einsum('bihk,bhkv->bihv', q_c, state)
        state = decay_c[..., None] * state + jnp.einsum('bjhk,bjhv->bhkv', k_c, v_c)
        return state, o

    xs = (jnp.moveaxis(q_dec, 1, 0), jnp.moveaxis(k_dec, 1, 0),
          jnp.moveaxis(vc, 1, 0), jnp.moveaxis(jnp.exp(b_end[:, :, 0]), 1, 0))
    s0 = jnp.zeros((b, GLA_HEADS, GLA_DK, GLA_DV), jnp.float32)
    _, o_inter = lax.scan(step, s0, xs)
    o = o_intra + jnp.moveaxis(o_inter, 0, 1)
    o = o * lax.rsqrt(jnp.mean(o * o, axis=-1, keepdims=True) + RMS_EPS)
    o = o.reshape(b, s, D_GLA) * head_norm.astype(jnp.float32)
    return o.astype(dt) * jax.nn.silu(g)


def mem_cross_attention(h, mem_h, w_q, w_kv, w_o):
    b, s, _ = h.shape
    q = (h @ w_q).reshape(b, s, XATTN_HEADS, XATTN_HEAD_DIM)
    kv = (mem_h @ w_kv).reshape(b, mem_h.shape[1], 2, XATTN_HEADS, XATTN_HEAD_DIM)
    k, v = kv[:, :, 0], kv[:, :, 1]
    logits = jnp.einsum('bshd,bmhd->bhsm', q, k).astype(jnp.float32) * (XATTN_HEAD_DIM ** -0.5)
    p = jax.nn.softmax(logits, axis=-1).astype(v.dtype)
    o = jnp.einsum('bhsm,bmhd->bshd', p, v).reshape(b, s, D_MODEL)
    return o @ w_o


def setup_inputs(seed: int = 0) -> dict:
    key = jax.random.key(seed)
    ks = jax.random.split(key, 32)

    def dense(k, shape, fan_in):
        return jax.random.normal(k, shape, jnp.float32) * (fan_in ** -0.5)

    def gain(k, shape):
        return 1.0 + 0.02 * jax.random.normal(k, shape, jnp.float32)

    L = DEPTH
    return {
        "x": jax.random.normal(ks[0], (BATCH, SEQ, D_MODEL), jnp.float32),
        "mem": jax.random.normal(ks[1], (BATCH, MEM_LEN, D_MODEL), jnp.float32),
        "ffn1_norm": gain(ks[2], (L, D_MODEL)),
        "ffn1_w_gate": dense(ks[3], (L, D_MODEL, D_FF), D_MODEL),
        "ffn1_w_up": dense(ks[4], (L, D_MODEL, D_FF), D_MODEL),
        "ffn1_w_down": dense(ks[5], (L, D_FF, D_MODEL), D_FF),
        "mix_norm": gain(ks[6], (L, D_MODEL)),
        "w_in": dense(ks[7], (L, D_MODEL, IN_COLS), D_MODEL),
        "pool_w": dense(ks[8], (L, N_POOL_GROUPS, POOL_GROUP, POOL_GROUP), POOL_GROUP),
        "pool_scale": 1.0 + 0.1 * jax.random.normal(ks[9], (L, D_POOL), jnp.float32),
        "gla_w_a2": dense(ks[10], (L, GLA_GATE_RANK, D_GLA_K), GLA_GATE_RANK),
        "gla_b_a": 0.1 * jax.random.normal(ks[11], (L, D_GLA_K), jnp.float32),
        "gla_head_norm": gain(ks[12], (L, D_GLA)),
        "w_out": dense(ks[13], (L, D_MIX, D_MODEL), D_MIX),
        "xattn_norm": gain(ks[14], (L, D_MODEL)),
        "mem_norm": gain(ks[15], (L, D_MODEL)),
        "xattn_w_q": dense(ks[16], (L, D_MODEL, D_MODEL), D_MODEL),
        "xattn_w_kv": dense(ks[17], (L, D_MODEL, 2 * D_MODEL), D_MODEL),
        "xattn_w_o": dense(ks[18], (L, D_MODEL, D_MODEL), D_MODEL),
        "ffn2_norm": gain(ks[19], (L, D_MODEL)),
        "ffn2_w_gate": dense(ks[20], (L, D_MODEL, D_FF), D_MODEL),
        "ffn2_w_up": dense(ks[21], (L, D_MODEL, D_FF), D_MODEL),
        "ffn2_w_down": dense(ks[22], (L, D_FF, D_MODEL), D_FF),
        "final_norm": gain(ks[23], (D_MODEL,)),
    }


def reference(x, mem, ffn1_norm, ffn1_w_gate, ffn1_w_up, ffn1_w_down, mix_norm, w_in,
              pool_w, pool_scale, gla_w_a2, gla_b_a, gla_head_norm, w_out,
              xattn_norm, mem_norm, xattn_w_q, xattn_w_kv, xattn_w_o,
              ffn2_norm, ffn2_w_gate, ffn2_w_up, ffn2_w_down, final_norm):
    for l in range(DEPTH):
        x = x + 0.5 * swiglu(rms_norm(x, ffn1_norm[l]), ffn1_w_gate[l], ffn1_w_up[l], ffn1_w_down[l])
        h = rms_norm(x, mix_norm[l])
        proj = h @ w_in[l]
        u, q, k, v, g, a_lr = jnp.split(proj, list(IN_SPLITS), axis=-1)
        y_pool = pool_mixer(u, pool_w[l], pool_scale[l])
        y_gla = gla_mixer(q, k, v, g, a_lr, gla_w_a2[l], gla_b_a[l], gla_head_norm[l])
        x = x + jnp.concatenate([y_pool, y_gla], axis=-1) @ w_out[l]
        x = x + mem_cross_attention(rms_norm(x, xattn_norm[l]), rms_norm(mem, mem_norm[l]),
                                    xattn_w_q[l], xattn_w_kv[l], xattn_w_o[l])
        x = x + 0.5 * swiglu(rms_norm(x, ffn2_norm[l]), ffn2_w_gate[l], ffn2_w_up[l], ffn2_w_down[l])
    return rms_norm(x, final_norm)
```

```python
import numpy as np
from contextlib import ExitStack
import concourse.bass as bass
import concourse.mybir as mybir
from concourse.bass_utils import run_bass_kernel_spmd

F32 = mybir.dt.float32
BF16 = mybir.dt.bfloat16
AF = mybir.ActivationFunctionType
ALU = mybir.AluOpType
AX = mybir.AxisListType

NCORES = 8
TOK = 2048
T = 512
NT = TOK // T
D = 2048
DC = 16
FF = 5632
FCH = 44
EPS = 1e-6
CB = 256
SLOT = 16 * CB
NSLOT = 4
PAYW = 1024 + 4 + 128

C_GAIN = 0
C_PSCALE = 96
C_HNORM = 104
C_WA2 = 1128
C_MTRI = 1640
C_IND = 1768
C_IDENT = 1770
C_M = 1898
C_ONEM = 1906
C_HM = 1914
C_PTAB = 1922
NCST = 1986

DEBUG_STAGE = None


class Sem:
    def __init__(self, h):
        self.h = h
        self.n = 0


class Buf:
    __slots__ = ("w", "r")

    def __init__(self, init=None):
        self.w = dict(init) if init else {}
        self.r = {}


def _merge(dst, src):
    for s, c in src.items():
        if dst.get(s, 0) < c:
            dst[s] = c


class Queue:
    def __init__(self, name, sem):
        self.name = name
        self.sem = sem
        self.prog = []
        self.waited = {}
        self.is_pe = name == "pe"

    def _wait(self, s, c):
        if s is self.sem and self.is_pe:
            return
        if self.waited.get(s, 0) >= c:
            return
        self.waited[s] = c
        self.prog.append(("wait", s, c))

    def op(self, fn, reads=(), writes=(), signal=True, dma_sem=None):
        for b in reads:
            for s, c in b.w.items():
                self._wait(s, c)
        for b in writes:
            for s, c in b.w.items():
                self._wait(s, c)
            for s, c in b.r.items():
                self._wait(s, c)
        if dma_sem is not None:
            if dma_sem.n:
                self._wait(dma_sem, dma_sem.n)
            dma_sem.n += 16
            ev = (dma_sem, dma_sem.n)
            self.prog.append(("dma", fn, dma_sem))
        elif signal:
            self.sem.n += 1
            ev = (self.sem, self.sem.n)
            self.prog.append(("op", fn, True))
        else:
            ev = (self.sem, self.sem.n + 1)
            self.prog.append(("op", fn, False))
        for b in reads:
            if b.r.get(ev[0], 0) < ev[1]:
                b.r[ev[0]] = ev[1]
        for b in writes:
            b.w = {ev[0]: ev[1]}
            b.r = {}
        return ev

    def replay(self, eng):
        for it in self.prog:
            if it[0] == "wait":
                eng.wait_ge(it[1].h, it[2])
            elif it[0] == "op":
                ins = it[1](eng)
                if it[2]:
                    ins.then_inc(self.sem.h, 1)
            else:
                ins = it[1](eng)
                ins.then_inc(it[2].h, 16)


class Arena:
    def __init__(self, cap):
        self.cap = cap
        self.top = 0
        self.dead = []
        self.live = []

    def alloc(self, nf32):
        nf32 = (nf32 + 15) // 16 * 16
        lo = self.top
        self.top += nf32
        assert self.top <= self.cap, ("SBUF arena overflow", self.top, self.cap)
        return lo

    def newbuf(self, lo, hi, track=True):
        init = {}
        for (a, b, ev) in self.dead:
            if a < hi and lo < b:
                _merge(init, ev)
        bf = Buf(init)
        if track:
            self.live.append((lo, hi, bf))
        return bf

    def mark(self):
        return self.top

    def release(self, mark):
        keep = []
        for (a, b, bf) in self.live:
            if a >= mark:
                ev = {}
                _merge(ev, bf.w)
                _merge(ev, bf.r)
                self.dead.append((a, b, ev))
            else:
                keep.append((a, b, bf))
        self.live = keep
        self.dead = [d for d in self.dead if d[0] < self.cap]
        self.top = mark


class TT:
    def __init__(self, ap, bufs):
        self.ap = ap
        self.bufs = bufs


def build_program():
    nc = bass.Bass("TRN2", target_bir_lowering=False)

    def din(name, shape):
        return nc.dram_tensor(name, list(shape), F32, kind="ExternalInput").ap()

    xT_d = din("xT", [128, DC, 4 * TOK])
    memT_d = din("memT", [128, DC, 256])
    cst_d = din("cst", [128, NCST])
    wg_d = [din("wg1", [22, 128, SLOT]), din("wg2", [22, 128, SLOT])]
    wu_d = [din("wu1", [22, 128, SLOT]), din("wu2", [22, 128, SLOT])]
    wd_d = [din("wd1", [32, 128, 11 * CB]), din("wd2", [32, 128, 11 * CB])]
    win_d = din("win", [16, 128, SLOT])
    wina_d = din("wina", [128, 256])
    poolw_d = din("poolw", [128, 2048])
    wout_d = din("wout", [8, 128, SLOT])
    wq_d = din("wq", [8, 128, SLOT])
    wkv_d = din("wkv", [16, 128, SLOT])
    wo_d = din("wo", [8, 128, SLOT])
    outT_d = nc.dram_tensor("outT", [128, DC, TOK], F32, kind="ExternalOutput").ap()

    ARENA_F32 = 52736
    es = ExitStack()
    arena_t = es.enter_context(nc.sbuf_tensor("arena", [128, ARENA_F32], F32))
    psum_t = [es.enter_context(nc.psum_tensor(f"ps{i}", [128, 512], F32)) for i in range(8)]

    def newsem(name):
        return Sem(es.enter_context(nc.semaphore(name)))

    pe = Queue("pe", newsem("s_pe"))
    act = Queue("act", newsem("s_act"))
    dve = Queue("dve", newsem("s_dve"))
    pool = Queue("pool", newsem("s_pool"))
    sp = Queue("sp", newsem("s_sp"))
    sp_sems = [newsem(f"s_spd{i}") for i in range(6)]
    sp_rr = [0]
    slot_sems = [newsem(f"s_slot{i}") for i in range(NSLOT)]
    cc_sem = newsem("s_cc")
    misc_sem = newsem("s_miscdma")

    A = Arena(ARENA_F32)

    def view(lo, shape, dtype):
        n = int(np.prod(shape[1:]))
        nf = n if dtype == F32 else n // 2
        ap = arena_t[:, lo:lo + nf]
        if dtype != F32:
            ap = ap.bitcast(dtype)
        if len(shape) == 3:
            ap = ap.rearrange("p (a b) -> p a b", b=shape[2])
        elif len(shape) == 4:
            ap = ap.rearrange("p (a b c) -> p a b c", b=shape[2], c=shape[3])
        return ap

    def alloc(shape, dtype, nbufs=1):
        n = int(np.prod(shape[1:]))
        nf = n if dtype == F32 else (n + 1) // 2
        lo = A.alloc(nf)
        ap = view(lo, shape, dtype)
        nfa = (nf + 15) // 16 * 16
        if nbufs == 1:
            bufs = [A.newbuf(lo, lo + nfa)]
        else:
            per = nf // nbufs
            bufs = [A.newbuf(lo + i * per, lo + (i + 1) * per) for i in range(nbufs)]
        return TT(ap, bufs)

    def halves(tt):
        tt.bufs = [tt.bufs[0], Buf(dict(tt.bufs[0].w))]
        return tt

    ps_bufs = [Buf() for _ in range(8)]
    ps_rr = [0]
    ps_pinned = set()

    def ps_next(pin=False):
        while True:
            i = ps_rr[0] % 8
            ps_rr[0] += 1
            if i not in ps_pinned:
                break
        if pin:
            ps_pinned.add(i)
        return psum_t[i], ps_bufs[i]

    def ps_unpin(pb):
        ps_pinned.discard(ps_bufs.index(pb))

    def sp_dma(out, in_, reads, writes):
        s = sp_sems[sp_rr[0] % len(sp_sems)]
        sp_rr[0] += 1
        return sp.op(lambda e, o=out, i=in_: e.dma_start(out=o, in_=i), reads, writes, dma_sem=s)


    cst = alloc([128, NCST], F32)
    ones_bf = alloc([128, 128], BF16)
    wa_bf = alloc([128, 16, 128], BF16)
    TOP_KT = ARENA_F32 - 2048
    TOP_VT = TOP_KT - 2048
    TOP_PW = TOP_VT - 1024
    KT = TT(view(TOP_KT, [128, 16, 256], BF16), None)
    Vt = TT(view(TOP_VT, [128, 2, 2048], BF16), None)
    poolw = TT(view(TOP_PW, [128, 8, 256], BF16), None)

    def activate_top(tt, lo, n):
        assert A.top <= lo, ("top region still in use by the stack", A.top, lo)
        A.cap = min(A.cap, lo)
        tt.bufs = [A.newbuf(lo, lo + n, track=False)]
    xT = alloc([128, DC, T], F32, nbufs=DC)
    hT = alloc([128, DC, T], BF16, nbufs=DC)
    slots = [alloc([128, SLOT], BF16) for _ in range(NSLOT)]
    Sst = alloc([128, 4, 256], F32, nbufs=4)
    Sst2 = alloc([128, 4, 256], F32, nbufs=4)
    halo = alloc([128, 8, 16], F32)
    rstd = alloc([128, T], F32)
    sq = [alloc([128, T], BF16) for _ in range(3)]
    slot_rr = [0]

    cap = cst.ap

    def gain_ap(gidx, c):
        return cap[:, C_GAIN + gidx * 16 + c: C_GAIN + gidx * 16 + c + 1]

    ident = cap[:, C_IDENT:C_IDENT + 128]

    def wget(dram_blk, n):
        i = slot_rr[0] % NSLOT
        slot_rr[0] += 1
        sl = slots[i]
        pool.op(lambda e, o=sl.ap[:, :n], src=dram_blk: e.dma_start(out=o, in_=src),
                reads=(), writes=sl.bufs, dma_sem=slot_sems[i])
        return sl

    def wview(sl, kn, cb=CB):
        return sl.ap[:, :kn * cb].rearrange("p (k c) -> p k c", c=cb)

    def mm(out, lhsT, rhs, start, stop, reads, writes, signal=None):
        if signal is None:
            signal = stop
        return pe.op(lambda e, o=out, l=lhsT, r=rhs, st=start, sp_=stop: e.matmul(o, l, r, start=st, stop=sp_),
                     reads, writes, signal)

    def tr(out, in_, idn, reads, writes, signal=True):
        return pe.op(lambda e, o=out, i=in_, d=idn: e.transpose(o, i, d), reads, writes, signal)

    def actf(out, in_, func, reads, writes, scale=1.0, bias=0.0, accum=None):
        def f(e, o=out, i=in_, fn=func, sc=scale, bi=bias, ac=accum):
            kw = {}
            if ac is not None:
                kw["accum_out"] = ac
            return e.activation(out=o, in_=i, func=fn, scale=sc, bias=bi, **kw)
        return act.op(f, reads, writes)

    def ts(out, in0, s1, s2, op0, op1, reads, writes):
        if op1 is None:
            return dve.op(lambda e, o=out, i=in0, a=s1, p0=op0: e.tensor_scalar(out=o, in0=i, scalar1=a, scalar2=None, op0=p0),
                          reads, writes)
        return dve.op(lambda e, o=out, i=in0, a=s1, b=s2, p0=op0, p1=op1:
                      e.tensor_scalar(out=o, in0=i, scalar1=a, scalar2=b, op0=p0, op1=p1), reads, writes)

    def stt(out, in0, scalar, in1, op0, op1, reads, writes):
        return dve.op(lambda e, o=out, i=in0, s=scalar, j=in1, p0=op0, p1=op1:
                      e.scalar_tensor_tensor(out=o, in0=i, scalar=s, in1=j, op0=p0, op1=p1), reads, writes)

    def tten(out, in0, in1, op, reads, writes):
        return dve.op(lambda e, o=out, i=in0, j=in1, p=op: e.tensor_tensor(out=o, in0=i, in1=j, op=p), reads, writes)

    def vcopy(out, in_, reads, writes):
        return dve.op(lambda e, o=out, i=in_: e.tensor_copy(out=o, in_=i), reads, writes)

    def vmemset(ap, val, writes):
        return dve.op(lambda e, a=ap, v=val: e.memset(a, v), (), writes)

    def rms_stats(src, nch, N, inv_n):
        pt, pb = ps_next()
        for c in range(nch):
            s = sq[c % 3]
            actf(s.ap[:, :N], src.ap[:, c, :N], AF.Square, [src.bufs[c]], s.bufs)
            mm(pt[:, :N], ones_bf.ap, s.ap[:, :N], c == 0, c == nch - 1, [ones_bf.bufs[0], s.bufs[0]], [pb], signal=True)
        actf(rstd.ap[:, :N], pt[:, :N], AF.Ln, [pb], rstd.bufs, scale=inv_n, bias=EPS)
        actf(rstd.ap[:, :N], rstd.ap[:, :N], AF.Exp, rstd.bufs, rstd.bufs, scale=-0.5)

    class StatAcc:
        def __init__(self, src, N=T, dst_rstd=None):
            self.src, self.N = src, N
            self.dst = rstd if dst_rstd is None else dst_rstd
            self.pt, self.pb = ps_next(pin=True)
            self.n = 0
            self.pending = []

        def push(self, c):
            self.pending.append(c)

        def flush(self):
            for c in self.pending:
                s = sq[self.n % 3]
                actf(s.ap[:, :self.N], self.src.ap[:, c, :self.N], AF.Square, [self.src.bufs[c]], s.bufs)
                mm(self.pt[:, :self.N], ones_bf.ap, s.ap[:, :self.N], self.n == 0, self.n == DC - 1,
                   [ones_bf.bufs[0], s.bufs[0]], [self.pb], signal=True)
                self.n += 1
            self.pending = []

        def finish(self):
            self.flush()
            assert self.n == DC
            N = self.N
            actf(self.dst.ap[:, :N], self.pt[:, :N], AF.Ln, [self.pb], self.dst.bufs, scale=1.0 / D, bias=EPS)
            actf(self.dst.ap[:, :N], self.dst.ap[:, :N], AF.Exp, self.dst.bufs, self.dst.bufs, scale=-0.5)
            ps_unpin(self.pb)

    def norm_to(src, gidx, dst, N, stats=None):
        if stats is None:
            rms_stats(src, DC, N, 1.0 / D)
            stats = rstd
        for c in range(DC):
            stt(dst.ap[:, c, :N], src.ap[:, c, :N], gain_ap(gidx, c), stats.ap[:, :N], ALU.mult, ALU.mult,
                [src.bufs[c], stats.bufs[0], cst.bufs[0]], [dst.bufs[c]])

    def ffn(which, xsrc=None):
        xsrc = xT if xsrc is None else xsrc
        mark = A.mark()
        aT = alloc([128, FCH, T], BF16, nbufs=FCH)
        sg = [alloc([128, T], F32) for _ in range(2)]
        for fb in range(22):
            wgs = wget(wg_d[which][fb], SLOT)
            wus = wget(wu_d[which][fb], SLOT)
            wgv, wuv = wview(wgs, 16), wview(wus, 16)
            if fb == 0:
                banks = [ps_next() for _ in range(4)]
                grp = [(wgv, wgs, 0), (wuv, wus, 0), (wgv, wgs, 1), (wuv, wus, 1)]
                for kc in range(DC):
                    for gi, (wv_, sl_, fc) in enumerate(grp):
                        mm(banks[gi][0][:, :], wv_[:, kc, fc * 128:(fc + 1) * 128], hT.ap[:, kc, :], kc == 0, kc == DC - 1,
                           [sl_.bufs[0], hT.bufs[kc]], [banks[gi][1]])
                for fc in range(2):
                    (pg, pgb), (pu, pub) = banks[2 * fc], banks[2 * fc + 1]
                    s_ = sg[fc % 2]
                    actf(s_.ap, pg[:, :], AF.Silu, [pgb], s_.bufs)
                    tten(aT.ap[:, fc, :], s_.ap, pu[:, :], ALU.mult, [s_.bufs[0], pub], [aT.bufs[fc]])
                continue
            for fc in range(2):
                ff = fb * 2 + fc
                pg, pgb = ps_next()
                pu, pub = ps_next()
                for kc in range(DC):
                    mm(pg[:, :], wgv[:, kc, fc * 128:(fc + 1) * 128], hT.ap[:, kc, :], kc == 0, kc == DC - 1,
                       [wgs.bufs[0], hT.bufs[kc]], [pgb])
                for kc in range(DC):
                    mm(pu[:, :], wuv[:, kc, fc * 128:(fc + 1) * 128], hT.ap[:, kc, :], kc == 0, kc == DC - 1,
                       [wus.bufs[0], hT.bufs[kc]], [pub])
                s = sg[ff % 2]
                actf(s.ap, pg[:, :], AF.Silu, [pgb], s.bufs)
                tten(aT.ap[:, ff, :], s.ap, pu[:, :], ALU.mult, [s.bufs[0], pub], [aT.bufs[ff]])
        acc = StatAcc(xT)
        for nb in range(8):
            pds = [ps_next() for _ in range(2)]
            for kg in range(4):
                if kg == 1:
                    acc.flush()
                wds = wget(wd_d[which][nb * 4 + kg], 11 * CB)
                wdv = wview(wds, 11)
                for dc in range(2):
                    for k in range(11):
                        mm(pds[dc][0][:, :], wdv[:, k, dc * 128:(dc + 1) * 128], aT.ap[:, kg * 11 + k, :],
                           kg == 0 and k == 0, kg == 3 and k == 10,
                           [wds.bufs[0], aT.bufs[kg * 11 + k]], [pds[dc][1]], signal=(k == 10))
            for dc in range(2):
                c = nb * 2 + dc
                stt(xT.ap[:, c, :], pds[dc][0][:, :], 0.5, xsrc.ap[:, c, :], ALU.mult, ALU.add,
                    [pds[dc][1], xsrc.bufs[c]], [xT.bufs[c]])
                acc.push(c)
        acc.finish()
        A.release(mark)

    def proj_feature_major_gen(w_d, blk0, nblk, rhs, evac, interleave_first=False):
        for nb in range(nblk):
            sl = wget(w_d[blk0 + nb], SLOT)
            wv = wview(sl, 16)
            if interleave_first and nb == 0:
                banks = [ps_next() for _ in range(2)]
                for kc in range(DC):
                    for dc in range(2):
                        mm(banks[dc][0][:, :], wv[:, kc, dc * 128:(dc + 1) * 128], rhs.ap[:, kc, :], kc == 0, kc == DC - 1,
                           [sl.bufs[0], rhs.bufs[kc]], [banks[dc][1]])
                for dc in range(2):
                    evac(nb * 2 + dc, banks[dc][0], banks[dc][1])
                    yield
                continue
            for dc in range(2):
                pt, pb = ps_next()
                for kc in range(DC):
                    mm(pt[:, :], wv[:, kc, dc * 128:(dc + 1) * 128], rhs.ap[:, kc, :], kc == 0, kc == DC - 1,
                       [sl.bufs[0], rhs.bufs[kc]], [pb])
                evac(nb * 2 + dc, pt, pb)
                yield

    def proj_feature_major(w_d, blk0, nblk, rhs, evac, interleave_first=False):
        for _ in proj_feature_major_gen(w_d, blk0, nblk, rhs, evac, interleave_first):
            pass

    def accum_into_x(w_d, rhs):
        acc = StatAcc(xT)

        def ev(ch, pt, pb):
            tten(xT.ap[:, ch, :], pt[:, :], xT.ap[:, ch, :], ALU.add, [pb, xT.bufs[ch]], [xT.bufs[ch]])
            acc.push(ch)
        for i, _ in enumerate(proj_feature_major_gen(w_d, 0, 8, rhs, ev)):
            if i % 2 == 1 and len(acc.pending) > 2:
                keep = acc.pending[-2:]
                acc.pending = acc.pending[:-2]
                acc.flush()
                acc.pending = keep
        acc.finish()

    def proj_token_major(w_d, blk0, nblk, evac, interleave_first=False):
        for nb in range(nblk):
            sl = wget(w_d[blk0 + nb], SLOT)
            wv = wview(sl, 16)
            if interleave_first and nb == 0:
                banks = [ps_next() for _ in range(4)]
                for kc in range(DC):
                    for tb in range(4):
                        mm(banks[tb][0][:, :CB], hT.ap[:, kc, tb * 128:(tb + 1) * 128], wv[:, kc, :], kc == 0, kc == DC - 1,
                           [sl.bufs[0], hT.bufs[kc]], [banks[tb][1]])
                for tb in range(4):
                    evac(nb, tb, banks[tb][0], banks[tb][1])
                continue
            for tb in range(4):
                pt, pb = ps_next()
                for kc in range(DC):
                    mm(pt[:, :CB], hT.ap[:, kc, tb * 128:(tb + 1) * 128], wv[:, kc, :], kc == 0, kc == DC - 1,
                       [sl.bufs[0], hT.bufs[kc]], [pb])
                evac(nb, tb, pt, pb)

    def gla_gates(mx):
        a_aug, l_t, expD, gam = mx["a_aug"], mx["l"], mx["expD"], mx["gam"]
        pt, pb = ps_next()
        for kc in range(DC):
            mm(pt[:, :], wa_bf.ap[:, kc, :], hT.ap[:, kc, :], kc == 0, kc == DC - 1,
               [wa_bf.bufs[0], hT.bufs[kc]], [pb])
        vmemset(a_aug.ap[0:32, :], 1.0, a_aug.bufs)
        actf(a_aug.ap[0:16, :], pt[0:16, :], AF.Copy, [pb], a_aug.bufs)
        pg, pgb = ps_next(pin=True)
        for tb in range(4):
            pz, pzb = ps_next()
            mm(pz[:, :], a_aug.ap[0:17, tb * 128:(tb + 1) * 128], cap[0:17, C_WA2:C_WA2 + 512], True, True,
               [a_aug.bufs[0], cst.bufs[0]], [pzb])
            actf(l_t.ap, pz[:, :], AF.Exp, [pzb], l_t.bufs, scale=-1.0)
            actf(l_t.ap, l_t.ap, AF.Ln, l_t.bufs, l_t.bufs, bias=1.0)
            pd, pdb = ps_next()
            mm(pd[:, :], cap[:, C_MTRI:C_MTRI + 128], l_t.ap, True, True, [cst.bufs[0], l_t.bufs[0]], [pdb])
            actf(expD.ap[:, tb, :], pd[:, :], AF.Exp, [pdb], [expD.bufs[tb]])
            for h in range(4):
                col = (tb * 4 + h) * 2
                mm(pg[:, col:col + 2], l_t.ap[:, h * 128:(h + 1) * 128], cap[:, C_IND:C_IND + 2], True, True,
                   [l_t.bufs[0], cst.bufs[0]], [pgb], signal=True)
        actf(gam.ap, pg[:, 0:32], AF.Exp, [pgb], gam.bufs)
        ps_unpin(pgb)

    def gla_kv_proj(mx, with_g, parts=("k", "v", "g")):
        expD, kdec, v = mx["expD"], mx["kdec"], mx["v"]

        def ev_k(nb, tb, pt, pb):
            tten(kdec.ap[:, tb, nb * CB:(nb + 1) * CB], pt[:, :CB], expD.ap[:, tb, nb * CB:(nb + 1) * CB], ALU.mult,
                 [pb, expD.bufs[tb]], [kdec.bufs[tb]])
        if "k" in parts:
            proj_token_major(win_d, 6, 2, ev_k)

        def ev_v(nb, tb, pt, pb):
            actf(v.ap[:, tb, nb * CB:(nb + 1) * CB], pt[:, :CB], AF.Copy, [pb], [v.bufs[tb]])
        if "v" in parts:
            proj_token_major(win_d, 8, 4, ev_v, interleave_first=True)
        if with_g and "g" in parts:
            G2, gt = mx["G2"], mx["gtmp"]

            def ev_g(nb, tb, pt, pb):
                g = gt[(nb * 4 + tb) % 2]
                actf(g.ap, pt[:, :CB], AF.Silu, [pb], g.bufs)
                tten(G2.ap[:, tb, nb * CB:(nb + 1) * CB], g.ap, cap[:, C_HNORM + nb * CB:C_HNORM + (nb + 1) * CB], ALU.mult,
                     [g.bufs[0], cst.bufs[0]], [G2.bufs[tb]])
            proj_token_major(win_d, 12, 4, ev_g)

    def gla_recur(mx, S2, tile_idx, full, filler=None):
        kdec, v, gam = mx["kdec"], mx["v"], mx["gam"]
        gam4 = gam.ap.rearrange("p (t h c) -> p t h c", h=4, c=2)

        def step():
            if filler is not None:
                next(filler, None)

        def stage_a(c):
            tb, half = c // 2, c % 2
            rows = slice(half * 64, half * 64 + 64)
            Sin, Sout = S2[c % 2], S2[(c + 1) % 2]
            banks = [ps_next(), ps_next()]
            for h in range(4):
                pt, pb = banks[h // 2]
                mm(pt[:, (h % 2) * 256:(h % 2) * 256 + 256], kdec.ap[rows, tb, h * 128:(h + 1) * 128],
                   v.ap[rows, tb, h * 256:(h + 1) * 256], True, True, [kdec.bufs[tb], v.bufs[tb]], [pb], signal=True)
            for h in range(4):
                pt, pb = banks[h // 2]
                stt(Sout.ap[:, h, :], Sin.ap[:, h, :], gam4[:, tb, h, half:half + 1], pt[:, (h % 2) * 256:(h % 2) * 256 + 256],
                    ALU.mult, ALU.add, [Sin.bufs[h], gam.bufs[0], pb], [Sout.bufs[h]])
            if full:
                Sbf = mx["Sbf"][c % 2]
                for h in range(4):
                    actf(Sbf.ap[:, h, :], Sout.ap[:, h, :], AF.Copy, [Sout.bufs[h]], [Sbf.bufs[h]])

        def stage_b(c):
            tb, half = c // 2, c % 2
            rows = slice(half * 64, half * 64 + 64)
            Sbf = mx["Sbf"][c % 2]
            qT, G2, y, ss, rs, junk, mixT = mx["qT"], mx["G2"], mx["y"], mx["ss"], mx["rs"], mx["junk"], mx["mixT"]
            yb, ssb, rsb, jb = [y.bufs[half]], [ss.bufs[half]], [rs.bufs[half]], [junk.bufs[half]]
            obanks = [ps_next(), ps_next()]
            for h in range(4):
                pt, pb = obanks[h // 2]
                mm(pt[:, (h % 2) * 256:(h % 2) * 256 + 256], qT.ap[:, h, tb * 128:(tb + 1) * 128], Sbf.ap[:, h, :],
                   True, True, [qT.bufs[0], Sbf.bufs[h]], [pb], signal=True)
            for h in range(4):
                pt, pb = obanks[h // 2]
                actf(junk.ap[rows, :], pt[rows, (h % 2) * 256:(h % 2) * 256 + 256], AF.Square, [pb], jb + ssb,
                     accum=ss.ap[rows, h:h + 1])
            actf(rs.ap[rows, :], ss.ap[rows, :], AF.Sqrt, ssb, rsb, scale=1.0 / 256, bias=EPS)
            dve.op(lambda e, o=rs.ap[rows, :]: e.reciprocal(out=o, in_=o), rsb, rsb)
            for h in range(4):
                pt, pb = obanks[h // 2]
                stt(y.ap[rows, h * 256:(h + 1) * 256], pt[rows, (h % 2) * 256:(h % 2) * 256 + 256], rs.ap[rows, h:h + 1],
                    G2.ap[rows, tb, h * 256:(h + 1) * 256], ALU.mult, ALU.mult, [pb, rsb[0], G2.bufs[tb]], yb)

        def stage_c(c):
            tb, half = c // 2, c % 2
            rows = slice(half * 64, half * 64 + 64)
            y, mixT = mx["y"], mx["mixT"]
            yb = [y.bufs[half]]
            ptt, ptb = ps_next()
            for fcx in range(8):
                tr(ptt[:, fcx * 64:(fcx + 1) * 64], y.ap[rows, fcx * 128:(fcx + 1) * 128], cap[rows, C_IDENT + half * 64:C_IDENT + half * 64 + 64],
                   [yb[0], cst.bufs[0]], [ptb], signal=(fcx == 7))
            actf(mixT.ap[:, 8:16, c * 64:(c + 1) * 64], ptt[:, :].rearrange("p (a b) -> p a b", b=64), AF.Copy,
                 [ptb], mixT.bufs[8:16])

        if not full:
            for c in range(8):
                stage_a(c)
                step()
            return
        stage_a(0)
        stage_a(1)
        stage_b(0)
        for c in range(8):
            if c + 2 < 8:
                stage_a(c + 2)
            step()
            if c + 1 < 8:
                stage_b(c + 1)
            step()
            stage_c(c)

    def pool_mixer(mx, tile_idx):
        u_ext, sA, sB, dd, mixT = mx["u_ext"], mx["sA"], mx["sB"], mx["d"], mx["mixT"]
        state = {}

        def ev_u(cc, pt, pb):
            g = cc // 2
            w = 2 ** (g + 1)
            ue = u_ext
            actf(ue.ap[:, 0:16], halo.ap[:, cc, :], AF.Copy, halo.bufs, ue.bufs)
            actf(ue.ap[:, 16:528], pt[:, :], AF.Copy, [pb], ue.bufs)
            actf(halo.ap[:, cc, :], ue.ap[:, 512:528], AF.Copy, ue.bufs, halo.bufs)
            cur, lo = ue, 0
            nxt = [sA, sB]
            for lev in range(g + 1):
                sh = 2 ** lev
                o = nxt[lev % 2]
                tten(o.ap[:, lo + sh:528], cur.ap[:, lo + sh:528], cur.ap[:, lo:528 - sh], ALU.add,
                     cur.bufs, o.bufs)
                cur, lo = o, lo + sh
            dslot = dd[g % 2]
            stt(dslot.ap[:, cc % 2, :], cur.ap[:, 16:528], 1.0 / w, ue.ap[:, 16:528], ALU.mult, ALU.subtract,
                cur.bufs + ue.bufs, dslot.bufs)
            if tile_idx == 0:
                tmp = mx["ptmp"]
                tten(tmp.ap, cur.ap[:, 16:32], cap[:, C_PTAB + g * 16:C_PTAB + (g + 1) * 16], ALU.mult,
                     cur.bufs + cst.bufs, tmp.bufs)
                tten(dslot.ap[:, cc % 2, 0:16], tmp.ap, ue.ap[:, 16:32], ALU.subtract, tmp.bufs + ue.bufs, dslot.bufs)
            if cc % 2 == 1:
                for oc in range(2):
                    pt2, pb2 = ps_next()
                    for kc in range(2):
                        mm(pt2[:, :], poolw.ap[:, g * 2 + kc, oc * 128:(oc + 1) * 128], dslot.ap[:, kc, :], kc == 0, kc == 1,
                           [poolw.bufs[0], dslot.bufs[0]], [pb2])
                    ch = g * 2 + oc
                    actf(mixT.ap[:, ch, :], pt2[:, :], AF.Copy, [pb2, cst.bufs[0]], [mixT.bufs[ch]],
                         scale=cap[:, C_PSCALE + ch:C_PSCALE + ch + 1])
        yield from proj_feature_major_gen(win_d, 0, 4, hT, ev_u)

    def xattn():
        mark = A.mark()
        qx = alloc([128, DC, T], BF16, nbufs=DC)
        p2 = [alloc([128, 4, 256], F32) for _ in range(2)]
        pT = alloc([128, 2, 4, T], BF16)
        oT = alloc([128, DC, T], BF16, nbufs=DC)
        mxt2 = [alloc([128, 4], F32) for _ in range(2)]
        nmx2 = [alloc([128, 4], F32) for _ in range(2)]
        sm2 = [alloc([128, 4], F32) for _ in range(2)]
        rsm2 = [alloc([128, 4], F32) for _ in range(2)]
        norm_to(xT, 2, hT, T, stats=rstd)

        def ev_q(ch, pt, pb):
            actf(qx.ap[:, ch, :], pt[:, :], AF.Copy, [pb], [qx.bufs[ch]], scale=float(512 ** -0.5))
        proj_feature_major(wq_d, 0, 8, hT, ev_q, interleave_first=True)

        lbanks = {}

        def emit_logits(tb):
            banks = [ps_next(), ps_next()]
            lbanks[tb] = banks
            for h in range(4):
                pt, pb = banks[h // 2]
                for dc in range(4):
                    mm(pt[:, (h % 2) * 256:(h % 2) * 256 + 256], qx.ap[:, h * 4 + dc, tb * 128:(tb + 1) * 128],
                       KT.ap[:, h * 4 + dc, :], dc == 0, dc == 3, [qx.bufs[h * 4 + dc], KT.bufs[0]], [pb])

        def emit_softmax(tb):
            banks = lbanks.pop(tb)
            p, mxt, nmx, sm, rsm = p2[tb % 2], mxt2[tb % 2], nmx2[tb % 2], sm2[tb % 2], rsm2[tb % 2]
            for bk in range(2):
                pt, pb = banks[bk]
                dve.op(lambda e, o=mxt.ap[:, bk * 2:bk * 2 + 2], i=pt[:, :].rearrange("p (a b) -> p a b", b=256):
                       e.tensor_reduce(out=o, in_=i, axis=AX.X, op=ALU.max), [pb], mxt.bufs)
            ts(nmx.ap, mxt.ap, -1.0, None, ALU.mult, None, mxt.bufs, nmx.bufs)
            for h in range(4):
                pt, pb = banks[h // 2]
                actf(p.ap[:, h, :], pt[:, (h % 2) * 256:(h % 2) * 256 + 256], AF.Exp, [pb, nmx.bufs[0]], p.bufs,
                     bias=nmx.ap[:, h:h + 1], accum=sm.ap[:, h:h + 1])
                sm.bufs[0].w = dict(p.bufs[0].w)
            dve.op(lambda e, o=rsm.ap, i=sm.ap: e.reciprocal(out=o, in_=i), sm.bufs + p.bufs, rsm.bufs)
            for h in range(4):
                ts(p.ap[:, h, :], p.ap[:, h, :], rsm.ap[:, h:h + 1], None, ALU.mult, None, p.bufs + rsm.bufs, p.bufs)
            tbanks = [ps_next(), ps_next()]
            for mc in range(2):
                pt, pb = tbanks[mc]
                for h in range(4):
                    tr(pt[:, h * 128:(h + 1) * 128], p.ap[:, h, mc * 128:(mc + 1) * 128], ident, [p.bufs[0], cst.bufs[0]], [pb],
                       signal=(h == 3))
                actf(pT.ap[:, mc, :, tb * 128:(tb + 1) * 128], pt[:, :].rearrange("p (a b) -> p a b", b=128), AF.Copy,
                     [pb], pT.bufs)

        emit_logits(0)
        for tb in range(4):
            if tb + 1 < 4:
                emit_logits(tb + 1)
            emit_softmax(tb)
        for ch in range(DC):
            h, dc = ch // 4, ch % 4
            pt, pb = ps_next()
            for mc in range(2):
                mm(pt[:, :], Vt.ap[:, mc, h * 512 + dc * 128:h * 512 + (dc + 1) * 128], pT.ap[:, mc, h, :], mc == 0, mc == 1,
                   [Vt.bufs[0], pT.bufs[0]], [pb])
            actf(oT.ap[:, ch, :], pt[:, :], AF.Copy, [pb], [oT.bufs[ch]])

        accum_into_x(wo_d, oT)
        A.release(mark)

    sp.op(lambda e: e.dma_start(out=cst.ap, in_=cst_d), (), cst.bufs, dma_sem=misc_sem)
    vmemset(ones_bf.ap, 1.0, ones_bf.bufs)
    vmemset(wa_bf.ap.rearrange("p a b -> p (a b)"), 0.0, wa_bf.bufs)
    pool.op(lambda e: e.dma_start(out=wa_bf.ap[:, :, 0:16], in_=wina_d.rearrange("p (a b) -> p a b", b=16)), (), wa_bf.bufs, dma_sem=cc_sem)

    def kv_norm(mhT):
        xm = TT(xT.ap[:, :, 0:256], xT.bufs)
        sp_dma(xm.ap, memT_d, (), xT.bufs)
        norm_to(xm, 5, mhT, 256)

    def k_gen(mhT):
        for nb in range(8):
            sl = wget(wkv_d[nb], SLOT)
            wv = wview(sl, 16)
            for dc in range(2):
                pt, pb = ps_next()
                for kc in range(DC):
                    mm(pt[:, :256], wv[:, kc, dc * 128:(dc + 1) * 128], mhT.ap[:, kc, :], kc == 0, kc == DC - 1,
                       [sl.bufs[0], mhT.bufs[kc]], [pb])
                actf(KT.ap[:, nb * 2 + dc, :], pt[:, :256], AF.Copy, [pb], KT.bufs)
                yield

    def v_gen(mhT):
        for nb in range(8):
            sl = wget(wkv_d[8 + nb], SLOT)
            wv = wview(sl, 16)
            for mc in range(2):
                pt, pb = ps_next()
                for kc in range(DC):
                    mm(pt[:, :CB], mhT.ap[:, kc, mc * 128:(mc + 1) * 128], wv[:, kc, :], kc == 0, kc == DC - 1,
                       [sl.bufs[0], mhT.bufs[kc]], [pb])
                actf(Vt.ap[:, mc, nb * CB:(nb + 1) * CB], pt[:, :CB], AF.Copy, [pb], Vt.bufs)
                yield

    vmemset(Sst.ap, 0.0, Sst.bufs)
    vmemset(halo.ap, 0.0, halo.bufs)
    NPRE = 3 * NT
    xn_mark = A.mark()
    xnext = alloc([128, DC, T], F32, nbufs=DC)
    sp_dma(xnext.ap, xT_d[:, :, 0:T], (), xnext.bufs)
    have_n = False
    mhT = None
    for t in range(NPRE):
        norm_to(xnext, 0, hT, T, stats=rstd if have_n else None)
        ffn(0, xsrc=xnext)
        sp_dma(xnext.ap, xT_d[:, :, (t + 1) * T:(t + 2) * T], (), xnext.bufs)
        norm_to(xT, 1, hT, T, stats=rstd)
        filler = None
        if t == NPRE - 2:
            mh_mark = A.mark()
            mhT = alloc([128, DC, 256], BF16, nbufs=DC)
            activate_top(KT, TOP_KT, 2048)
            filler = k_gen(mhT)
        elif t == NPRE - 1:
            activate_top(Vt, TOP_VT, 2048)
            activate_top(poolw, TOP_PW, 1024)
            pool.op(lambda e: e.dma_start(out=poolw.ap.rearrange("p a b -> p (a b)"), in_=poolw_d), (), poolw.bufs, dma_sem=cc_sem)
            filler = v_gen(mhT)
        mark = A.mark()
        mx = {
            "a_aug": alloc([32, T], F32), "l": alloc([128, 512], F32),
            "expD": alloc([128, 4, 512], F32, nbufs=4), "gam": alloc([128, 32], F32),
            "kdec": alloc([128, 4, 512], BF16, nbufs=4), "v": alloc([128, 4, 1024], BF16, nbufs=4),
        }
        gla_kv_proj(mx, with_g=False, parts=("v",))
        gla_gates(mx)
        gla_kv_proj(mx, with_g=False, parts=("k",))
        if t == NPRE - 2:
            kv_norm(mhT)
        gla_recur(mx, [Sst, Sst2], t, full=False, filler=filler)
        if filler is not None:
            for _ in filler:
                pass
        if t == NPRE - 1:
            for nb in range(4):
                sl = wget(win_d[nb], SLOT)
                wv = wview(sl, 16)
                for dc in range(2):
                    pt, pb = ps_next()
                    for kc in range(DC):
                        mm(pt[:, 0:16], wv[:, kc, dc * 128:(dc + 1) * 128], hT.ap[:, kc, T - 16:T], kc == 0, kc == DC - 1,
                           [sl.bufs[0], hT.bufs[kc]], [pb])
                    cc = nb * 2 + dc
                    actf(halo.ap[:, cc, :], pt[:, 0:16], AF.Copy, [pb], halo.bufs)
        A.release(mark)
        if t == NPRE - 1:
            A.release(mh_mark)
        accn = StatAcc(xnext, T, rstd)
        for c in range(DC):
            accn.push(c)
        accn.finish()
        have_n = True

    for t in range(NT):
        tsl = slice(t * T, (t + 1) * T)
        norm_to(xnext, 0, hT, T, stats=rstd if t == 0 else None)
        ffn(0, xsrc=xnext)
        A.release(xn_mark)
        if DEBUG_STAGE == "x1":
            sp_dma(outT_d[:, :, tsl], xT.ap, xT.bufs, [Buf()])
            continue
        norm_to(xT, 1, hT, T, stats=rstd)
        mark = A.mark()
        mx = {
            "a_aug": alloc([32, T], F32), "l": alloc([128, 512], F32),
            "expD": alloc([128, 4, 512], F32, nbufs=4), "gam": alloc([128, 32], F32),
            "kdec": alloc([128, 4, 512], BF16, nbufs=4), "v": alloc([128, 4, 1024], BF16, nbufs=4),
            "G2": alloc([128, 4, 1024], BF16, nbufs=4), "gtmp": [alloc([128, CB], F32) for _ in range(2)],
            "mixT": alloc([128, DC, T], BF16, nbufs=DC), "qT": alloc([128, 4, T], BF16),
            "u_ext": alloc([128, 528], F32), "sA": alloc([128, 528], F32), "sB": alloc([128, 528], F32),
            "d": [alloc([128, 2, T], BF16) for _ in range(2)], "ptmp": alloc([128, 16], F32),
            "y": halves(alloc([128, 1024], F32)), "Sbf": [alloc([128, 4, 256], BF16, nbufs=4) for _ in range(2)],
            "junk": halves(alloc([128, 256], BF16)), "ss": halves(alloc([128, 4], F32)), "rs": halves(alloc([128, 4], F32)),
        }
        gla_kv_proj(mx, with_g=True, parts=("v",))
        gla_gates(mx)
        qT = mx["qT"]

        def ev_qg(ch, pt, pb, qT=qT):
            actf(qT.ap[:, ch, :], pt[:, :], AF.Copy, [pb], qT.bufs, scale=float(128 ** -0.5))
        proj_feature_major(win_d, 4, 2, hT, ev_qg)
        gla_kv_proj(mx, with_g=True, parts=("k", "g"))
        filler = pool_mixer(mx, t)
        gla_recur(mx, [Sst, Sst2], t, full=True, filler=filler)
        for _ in filler:
            pass
        mixT = mx["mixT"]

        accum_into_x(wout_d, mixT)
        A.release(mark)
        if DEBUG_STAGE == "x2":
            sp_dma(outT_d[:, :, tsl], xT.ap, xT.bufs, [Buf()])
            continue
        xattn()
        if DEBUG_STAGE == "x3":
            sp_dma(outT_d[:, :, tsl], xT.ap, xT.bufs, [Buf()])
            continue
        if t + 1 < NT:
            xn_mark = A.mark()
            xnext = alloc([128, DC, T], F32, nbufs=DC)
            sp_dma(xnext.ap, xT_d[:, :, (NPRE + t + 1) * T:(NPRE + t + 2) * T], (), xnext.bufs)
        norm_to(xT, 3, hT, T, stats=rstd)
        ffn(1)
        mark = A.mark()
        outb = alloc([128, DC, T], F32, nbufs=DC)
        for c in range(DC):
            stt(outb.ap[:, c, :], xT.ap[:, c, :], gain_ap(4, c), rstd.ap[:, :T], ALU.mult, ALU.mult,
                [xT.bufs[c], rstd.bufs[0], cst.bufs[0]], [outb.bufs[c]])
            if c % 4 == 3:
                sp_dma(outT_d[:, c - 3:c + 1, tsl], outb.ap[:, c - 3:c + 1, :], outb.bufs[c - 3:c + 1], [Buf()])
        A.release(mark)

    for s in sp_sems:
        if s.n:
            sp._wait(s, s.n)

    block = es.enter_context(nc.Block())

    @block.tensor
    def _(e):
        pe.replay(e)

    @block.scalar
    def _(e):
        act.replay(e)

    @block.vector
    def _(e):
        dve.replay(e)

    @block.gpsimd
    def _(e):
        pool.replay(e)

    @block.sync
    def _(e):
        sp.replay(e)

    es.close()
    return nc


def _tile_w(W, cb, kgroups=None):
    K, N = W.shape
    KC = K // 128
    if kgroups is None:
        kgroups = [KC]
    Wr = W.reshape(KC, 128, N // cb, cb)
    blocks = []
    for nb in range(N // cb):
        k0 = 0
        for kn in kgroups:
            blk = Wr[k0:k0 + kn, :, nb, :]
            blocks.append(np.ascontiguousarray(blk.transpose(1, 0, 2)).reshape(128, kn * cb))
            k0 += kn
    return np.ascontiguousarray(np.stack(blocks))


def _fm(v, nch):
    return np.ascontiguousarray(np.asarray(v, np.float32).reshape(nch, 128).T)


_PROGRAM = None


def kernel(x, mem, ffn1_norm, ffn1_w_gate, ffn1_w_up, ffn1_w_down, mix_norm, w_in,
           pool_w, pool_scale, gla_w_a2, gla_b_a, gla_head_norm, w_out,
           xattn_norm, mem_norm, xattn_w_q, xattn_w_kv, xattn_w_o,
           ffn2_norm, ffn2_w_gate, ffn2_w_up, ffn2_w_down, final_norm):
    global _PROGRAM
    f = lambda a: np.asarray(a, np.float32)
    x, mem = f(x), f(mem)
    shared = {
        "wg1": _tile_w(f(ffn1_w_gate)[0], CB), "wu1": _tile_w(f(ffn1_w_up)[0], CB),
        "wd1": _tile_w(f(ffn1_w_down)[0], CB, [11] * 4),
        "wg2": _tile_w(f(ffn2_w_gate)[0], CB), "wu2": _tile_w(f(ffn2_w_up)[0], CB),
        "wd2": _tile_w(f(ffn2_w_down)[0], CB, [11] * 4),
        "win": _tile_w(f(w_in)[0][:, :4096], CB),
        "wina": np.ascontiguousarray(f(w_in)[0][:, 4096:4112].reshape(16, 128, 16).transpose(1, 0, 2)).reshape(128, 256),
        "poolw": np.ascontiguousarray(f(pool_w)[0].reshape(4, 2, 128, 256).transpose(2, 0, 1, 3)).reshape(128, 2048),
        "wout": _tile_w(f(w_out)[0], CB), "wq": _tile_w(f(xattn_w_q)[0], CB),
        "wkv": _tile_w(f(xattn_w_kv)[0], CB), "wo": _tile_w(f(xattn_w_o)[0], CB),
    }
    cst = np.zeros((128, NCST), np.float32)
    gains = [ffn1_norm[0], mix_norm[0], xattn_norm[0], ffn2_norm[0], final_norm, mem_norm[0]]
    for i, g in enumerate(gains):
        cst[:, C_GAIN + i * 16:C_GAIN + (i + 1) * 16] = _fm(g, 16)
    cst[:, C_PSCALE:C_PSCALE + 8] = _fm(f(pool_scale)[0], 8)
    cst[:, C_HNORM:C_HNORM + 1024] = f(gla_head_norm)[0][None, :]
    cst[0:16, C_WA2:C_WA2 + 512] = f(gla_w_a2)[0]
    cst[16, C_WA2:C_WA2 + 512] = f(gla_b_a)[0]
    ii = np.arange(128)
    same = (ii[:, None] // 64) == (ii[None, :] // 64)
    cst[:, C_MTRI:C_MTRI + 128] = np.where(same & (ii[:, None] > ii[None, :]), -1.0 / 16.0, 0.0)
    cst[:, C_IND:C_IND + 2] = np.where((ii[:, None] // 64) == np.arange(2)[None, :], -1.0 / 16.0, 0.0)
    cst[:, C_IDENT:C_IDENT + 128] = np.eye(128, dtype=np.float32)

    in_maps = []
    for core in range(NCORES):
        b, q = core // 4, core % 4
        xs = np.zeros((4 * TOK, D), np.float32)
        xs[(3 - q) * TOK:] = x[b, 0:(q + 1) * TOK, :]
        xTc = np.ascontiguousarray(xs.reshape(4 * TOK, DC, 128).transpose(2, 1, 0))
        memTc = np.ascontiguousarray(mem[b].reshape(256, DC, 128).transpose(2, 1, 0))
        c = cst.copy()
        for r in range(NCORES):
            m = 1.0 if (r // 4 == b and r % 4 < q) else 0.0
            c[:, C_M + r] = m
            c[:, C_ONEM + r] = 1.0 - m
            c[:, C_HM + r] = 1.0 if (r == core - 1 and q > 0) else 0.0
        for g in range(4):
            w = 2 ** (g + 1)
            tt = np.arange(16)
            c[:, C_PTAB + g * 16:C_PTAB + (g + 1) * 16] = (1.0 / np.minimum(tt + 1, w) if q == 0 else np.full(16, 1.0 / w))[None, :]
        d = {"xT": xTc, "memT": memTc, "cst": c}
        d.update(shared)
        in_maps.append(d)

    if _PROGRAM is None:
        _PROGRAM = build_program()
    res = run_bass_kernel_spmd(_PROGRAM, in_maps, core_ids=list(range(NCORES)))
    out = np.empty((2, 8192, D), np.float32)
    for core in range(NCORES):
        b, q = core // 4, core % 4
        oT = np.asarray(res.results[core]["outT"])
        out[b, q * TOK:(q + 1) * TOK, :] = oT.transpose(2, 1, 0).reshape(TOK, D)
    return out
```

```python
import numpy as np
from contextlib import ExitStack
import concourse.bass as bass
import concourse.mybir as mybir
from concourse.bass_utils import run_bass_kernel_spmd

F32 = mybir.dt.float32
BF16 = mybir.dt.bfloat16
AF = mybir.ActivationFunctionType
ALU = mybir.AluOpType
AX = mybir.AxisListType

NCORES = 8
TOK = 2048
T = 512
NT = TOK // T
D = 2048
DC = 16
FF = 5632
FCH = 44
EPS = 1e-6
CB = 256
SLOT = 16 * CB
NSLOT = 4
PAYW = 1024 + 4 + 128

C_GAIN = 0
C_PSCALE = 96
C_HNORM = 104
C_WA2 = 1128
C_MTRI = 1640
C_IND = 1768
C_IDENT = 1770
C_M = 1898
C_ONEM = 1906
C_HM = 1914
C_PTAB = 1922
NCST = 1986

DEBUG_STAGE = None


class Sem:
    def __init__(self, h):
        self.h = h
        self.n = 0


class Buf:
    __slots__ = ("w", "r")

    def __init__(self, init=None):
        self.w = dict(init) if init else {}
        self.r = {}


def _merge(dst, src):
    for s, c in src.items():
        if dst.get(s, 0) < c:
            dst[s] = c


class Queue:
    def __init__(self, name, sem):
        self.name = name
        self.sem = sem
        self.prog = []
        self.waited = {}
        self.is_pe = name == "pe"

    def _wait(self, s, c):
        if s is self.sem and self.is_pe:
            return
        if self.waited.get(s, 0) >= c:
            return
        self.waited[s] = c
        self.prog.append(("wait", s, c))

    def op(self, fn, reads=(), writes=(), signal=True, dma_sem=None):
        for b in reads:
            for s, c in b.w.items():
                self._wait(s, c)
        for b in writes:
            for s, c in b.w.items():
                self._wait(s, c)
            for s, c in b.r.items():
                self._wait(s, c)
        if dma_sem is not None:
            if dma_sem.n:
                self._wait(dma_sem, dma_sem.n)
            dma_sem.n += 16
            ev = (dma_sem, dma_sem.n)
            self.prog.append(("dma", fn, dma_sem))
        elif signal:
            self.sem.n += 1
            ev = (self.sem, self.sem.n)
            self.prog.append(("op", fn, True))
        else:
            ev = (self.sem, self.sem.n + 1)
            self.prog.append(("op", fn, False))
        for b in reads:
            if b.r.get(ev[0], 0) < ev[1]:
                b.r[ev[0]] = ev[1]
        for b in writes:
            b.w = {ev[0]: ev[1]}
            b.r = {}
        return ev

    def replay(self, eng):
        for it in self.prog:
            if it[0] == "wait":
                eng.wait_ge(it[1].h, it[2])
            elif it[0] == "op":
                ins = it[1](eng)
                if it[2]:
                    ins.then_inc(self.sem.h, 1)
            else:
                ins = it[1](eng)
                ins.then_inc(it[2].h, 16)


class Arena:
    def __init__(self, cap):
        self.cap = cap
        self.top = 0
        self.dead = []
        self.live = []

    def alloc(self, nf32):
        nf32 = (nf32 + 15) // 16 * 16
        lo = self.top
        self.top += nf32
        assert self.top <= self.cap, ("SBUF arena overflow", self.top, self.cap)
        return lo

    def newbuf(self, lo, hi, track=True):
        init = {}
        for (a, b, ev) in self.dead:
            if a < hi and lo < b:
                _merge(init, ev)
        bf = Buf(init)
        if track:
            self.live.append((lo, hi, bf))
        return bf

    def mark(self):
        return self.top

    def release(self, mark):
        keep = []
        for (a, b, bf) in self.live:
            if a >= mark:
                ev = {}
                _merge(ev, bf.w)
                _merge(ev, bf.r)
                self.dead.append((a, b, ev))
            else:
                keep.append((a, b, bf))
        self.live = keep
        self.dead = [d for d in self.dead if d[0] < self.cap]
        self.top = mark


class TT:
    def __init__(self, ap, bufs):
        self.ap = ap
        self.bufs = bufs


def build_program():
    nc = bass.Bass("TRN2", target_bir_lowering=False)

    def din(name, shape):
        return nc.dram_tensor(name, list(shape), F32, kind="ExternalInput").ap()

    xT_d = din("xT", [128, DC, 4 * TOK])
    memT_d = din("memT", [128, DC, 256])
    cst_d = din("cst", [128, NCST])
    wg_d = [din("wg1", [22, 128, SLOT]), din("wg2", [22, 128, SLOT])]
    wu_d = [din("wu1", [22, 128, SLOT]), din("wu2", [22, 128, SLOT])]
    wd_d = [din("wd1", [32, 128, 11 * CB]), din("wd2", [32, 128, 11 * CB])]
    win_d = din("win", [16, 128, SLOT])
    wina_d = din("wina", [128, 256])
    poolw_d = din("poolw", [128, 2048])
    wout_d = din("wout", [8, 128, SLOT])
    wq_d = din("wq", [8, 128, SLOT])
    wkv_d = din("wkv", [16, 128, SLOT])
    wo_d = din("wo", [8, 128, SLOT])
    outT_d = nc.dram_tensor("outT", [128, DC, TOK], F32, kind="ExternalOutput").ap()

    ARENA_F32 = 52736
    es = ExitStack()
    arena_t = es.enter_context(nc.sbuf_tensor("arena", [128, ARENA_F32], F32))
    psum_t = [es.enter_context(nc.psum_tensor(f"ps{i}", [128, 512], F32)) for i in range(8)]

    def newsem(name):
        return Sem(es.enter_context(nc.semaphore(name)))

    pe = Queue("pe", newsem("s_pe"))
    act = Queue("act", newsem("s_act"))
    dve = Queue("dve", newsem("s_dve"))
    pool = Queue("pool", newsem("s_pool"))
    sp = Queue("sp", newsem("s_sp"))
    sp_sems = [newsem(f"s_spd{i}") for i in range(6)]
    sp_rr = [0]
    slot_sems = [newsem(f"s_slot{i}") for i in range(NSLOT)]
    cc_sem = newsem("s_cc")
    misc_sem = newsem("s_miscdma")

    A = Arena(ARENA_F32)

    def view(lo, shape, dtype):
        n = int(np.prod(shape[1:]))
        nf = n if dtype == F32 else n // 2
        ap = arena_t[:, lo:lo + nf]
        if dtype != F32:
            ap = ap.bitcast(dtype)
        if len(shape) == 3:
            ap = ap.rearrange("p (a b) -> p a b", b=shape[2])
        elif len(shape) == 4:
            ap = ap.rearrange("p (a b c) -> p a b c", b=shape[2], c=shape[3])
        return ap

    def alloc(shape, dtype, nbufs=1):
        n = int(np.prod(shape[1:]))
        nf = n if dtype == F32 else (n + 1) // 2
        lo = A.alloc(nf)
        ap = view(lo, shape, dtype)
        nfa = (nf + 15) // 16 * 16
        if nbufs == 1:
            bufs = [A.newbuf(lo, lo + nfa)]
        else:
            per = nf // nbufs
            bufs = [A.newbuf(lo + i * per, lo + (i + 1) * per) for i in range(nbufs)]
        return TT(ap, bufs)

    def halves(tt):
        tt.bufs = [tt.bufs[0], Buf(dict(tt.bufs[0].w))]
        return tt

    ps_bufs = [Buf() for _ in range(8)]
    ps_rr = [0]
    ps_pinned = set()

    def ps_next(pin=False):
        while True:
            i = ps_rr[0] % 8
            ps_rr[0] += 1
            if i not in ps_pinned:
                break
        if pin:
            ps_pinned.add(i)
        return psum_t[i], ps_bufs[i]

    def ps_unpin(pb):
        ps_pinned.discard(ps_bufs.index(pb))

    def sp_dma(out, in_, reads, writes):
        s = sp_sems[sp_rr[0] % len(sp_sems)]
        sp_rr[0] += 1
        return sp.op(lambda e, o=out, i=in_: e.dma_start(out=o, in_=i), reads, writes, dma_sem=s)


    cst = alloc([128, NCST], F32)
    ones_bf = alloc([128, 128], BF16)
    wa_bf = alloc([128, 16, 128], BF16)
    TOP_KT = ARENA_F32 - 2048
    TOP_VT = TOP_KT - 2048
    TOP_PW = TOP_VT - 1024
    KT = TT(view(TOP_KT, [128, 16, 256], BF16), None)
    Vt = TT(view(TOP_VT, [128, 2, 2048], BF16), None)
    poolw = TT(view(TOP_PW, [128, 8, 256], BF16), None)

    def activate_top(tt, lo, n):
        assert A.top <= lo, ("top region still in use by the stack", A.top, lo)
        A.cap = min(A.cap, lo)
        tt.bufs = [A.newbuf(lo, lo + n, track=False)]
    xT = alloc([128, DC, T], F32, nbufs=DC)
    hT = alloc([128, DC, T], BF16, nbufs=DC)
    slots = [alloc([128, SLOT], BF16) for _ in range(NSLOT)]
    Sst = alloc([128, 4, 256], F32, nbufs=4)
    Sst2 = alloc([128, 4, 256], F32, nbufs=4)
    halo = alloc([128, 8, 16], F32)
    rstd = alloc([128, T], F32)
    sq = [alloc([128, T], BF16) for _ in range(3)]
    slot_rr = [0]

    cap = cst.ap

    def gain_ap(gidx, c):
        return cap[:, C_GAIN + gidx * 16 + c: C_GAIN + gidx * 16 + c + 1]

    ident = cap[:, C_IDENT:C_IDENT + 128]

    def wget(dram_blk, n):
        i = slot_rr[0] % NSLOT
        slot_rr[0] += 1
        sl = slots[i]
        pool.op(lambda e, o=sl.ap[:, :n], src=dram_blk: e.dma_start(out=o, in_=src),
                reads=(), writes=sl.bufs, dma_sem=slot_sems[i])
        return sl

    def wview(sl, kn, cb=CB):
        return sl.ap[:, :kn * cb].rearrange("p (k c) -> p k c", c=cb)

    def mm(out, lhsT, rhs, start, stop, reads, writes, signal=None):
        if signal is None:
            signal = stop
        return pe.op(lambda e, o=out, l=lhsT, r=rhs, st=start, sp_=stop: e.matmul(o, l, r, start=st, stop=sp_),
                     reads, writes, signal)

    def tr(out, in_, idn, reads, writes, signal=True):
        return pe.op(lambda e, o=out, i=in_, d=idn: e.transpose(o, i, d), reads, writes, signal)

    def actf(out, in_, func, reads, writes, scale=1.0, bias=0.0, accum=None):
        def f(e, o=out, i=in_, fn=func, sc=scale, bi=bias, ac=accum):
            kw = {}
            if ac is not None:
                kw["accum_out"] = ac
            return e.activation(out=o, in_=i, func=fn, scale=sc, bias=bi, **kw)
        return act.op(f, reads, writes)

    def ts(out, in0, s1, s2, op0, op1, reads, writes):
        if op1 is None:
            return dve.op(lambda e, o=out, i=in0, a=s1, p0=op0: e.tensor_scalar(out=o, in0=i, scalar1=a, scalar2=None, op0=p0),
                          reads, writes)
        return dve.op(lambda e, o=out, i=in0, a=s1, b=s2, p0=op0, p1=op1:
                      e.tensor_scalar(out=o, in0=i, scalar1=a, scalar2=b, op0=p0, op1=p1), reads, writes)

    def stt(out, in0, scalar, in1, op0, op1, reads, writes):
        return dve.op(lambda e, o=out, i=in0, s=scalar, j=in1, p0=op0, p1=op1:
                      e.scalar_tensor_tensor(out=o, in0=i, scalar=s, in1=j, op0=p0, op1=p1), reads, writes)

    def tten(out, in0, in1, op, reads, writes):
        return dve.op(lambda e, o=out, i=in0, j=in1, p=op: e.tensor_tensor(out=o, in0=i, in1=j, op=p), reads, writes)

    def vcopy(out, in_, reads, writes):
        return dve.op(lambda e, o=out, i=in_: e.tensor_copy(out=o, in_=i), reads, writes)

    def vmemset(ap, val, writes):
        return dve.op(lambda e, a=ap, v=val: e.memset(a, v), (), writes)

    def rms_stats(src, nch, N, inv_n):
        pt, pb = ps_next()
        for c in range(nch):
            s = sq[c % 3]
            actf(s.ap[:, :N], src.ap[:, c, :N], AF.Square, [src.bufs[c]], s.bufs)
            mm(pt[:, :N], ones_bf.ap, s.ap[:, :N], c == 0, c == nch - 1, [ones_bf.bufs[0], s.bufs[0]], [pb], signal=True)
        actf(rstd.ap[:, :N], pt[:, :N], AF.Ln, [pb], rstd.bufs, scale=inv_n, bias=EPS)
        actf(rstd.ap[:, :N], rstd.ap[:, :N], AF.Exp, rstd.bufs, rstd.bufs, scale=-0.5)

    class StatAcc:
        def __init__(self, src, N=T, dst_rstd=None):
            self.src, self.N = src, N
            self.dst = rstd if dst_rstd is None else dst_rstd
            self.pt, self.pb = ps_next(pin=True)
            self.n = 0
            self.pending = []

        def push(self, c):
            self.pending.append(c)

        def flush(self):
            for c in self.pending:
                s = sq[self.n % 3]
                actf(s.ap[:, :self.N], self.src.ap[:, c, :self.N], AF.Square, [self.src.bufs[c]], s.bufs)
                mm(self.pt[:, :self.N], ones_bf.ap, s.ap[:, :self.N], self.n == 0, self.n == DC - 1,
                   [ones_bf.bufs[0], s.bufs[0]], [self.pb], signal=True)
                self.n += 1
            self.pending = []

        def finish(self):
            self.flush()
            assert self.n == DC
            N = self.N
            actf(self.dst.ap[:, :N], self.pt[:, :N], AF.Ln, [self.pb], self.dst.bufs, scale=1.0 / D, bias=EPS)
            actf(self.dst.ap[:, :N], self.dst.ap[:, :N], AF.Exp, self.dst.bufs, self.dst.bufs, scale=-0.5)
            ps_unpin(self.pb)

    def norm_to(src, gidx, dst, N, stats=None):
        if stats is None:
            rms_stats(src, DC, N, 1.0 / D)
            stats = rstd
        for c in range(DC):
            stt(dst.ap[:, c, :N], src.ap[:, c, :N], gain_ap(gidx, c), stats.ap[:, :N], ALU.mult, ALU.mult,
                [src.bufs[c], cst.bufs[0]] + list(stats.bufs), [dst.bufs[c]])

    def ffn(which, xsrc=None, mid_hook=None):
        xsrc = xT if xsrc is None else xsrc
        mark = A.mark()
        aT = alloc([128, FCH, T], BF16, nbufs=FCH)
        sg = [alloc([128, T], F32) for _ in range(2)]
        for fb in range(22):
            wgs = wget(wg_d[which][fb], SLOT)
            wus = wget(wu_d[which][fb], SLOT)
            wgv, wuv = wview(wgs, 16), wview(wus, 16)
            if fb == 0:
                banks = [ps_next() for _ in range(4)]
                grp = [(wgv, wgs, 0), (wuv, wus, 0), (wgv, wgs, 1), (wuv, wus, 1)]
                for kc in range(DC):
                    for gi, (wv_, sl_, fc) in enumerate(grp):
                        mm(banks[gi][0][:, :], wv_[:, kc, fc * 128:(fc + 1) * 128], hT.ap[:, kc, :], kc == 0, kc == DC - 1,
                           [sl_.bufs[0], hT.bufs[kc]], [banks[gi][1]])
                for fc in range(2):
                    (pg, pgb), (pu, pub) = banks[2 * fc], banks[2 * fc + 1]
                    s_ = sg[fc % 2]
                    actf(s_.ap, pg[:, :], AF.Silu, [pgb], s_.bufs)
                    tten(aT.ap[:, fc, :], s_.ap, pu[:, :], ALU.mult, [s_.bufs[0], pub], [aT.bufs[fc]])
                continue
            for fc in range(2):
                ff = fb * 2 + fc
                pg, pgb = ps_next()
                pu, pub = ps_next()
                for kc in range(DC):
                    mm(pg[:, :], wgv[:, kc, fc * 128:(fc + 1) * 128], hT.ap[:, kc, :], kc == 0, kc == DC - 1,
                       [wgs.bufs[0], hT.bufs[kc]], [pgb])
                for kc in range(DC):
                    mm(pu[:, :], wuv[:, kc, fc * 128:(fc + 1) * 128], hT.ap[:, kc, :], kc == 0, kc == DC - 1,
                       [wus.bufs[0], hT.bufs[kc]], [pub])
                s = sg[ff % 2]
                actf(s.ap, pg[:, :], AF.Silu, [pgb], s.bufs)
                tten(aT.ap[:, ff, :], s.ap, pu[:, :], ALU.mult, [s.bufs[0], pub], [aT.bufs[ff]])
        if mid_hook is not None:
            mid_hook()
        acc = StatAcc(xT)
        for nb in range(8):
            pds = [ps_next() for _ in range(2)]
            for kg in range(4):
                if kg == 1:
                    acc.flush()
                wds = wget(wd_d[which][nb * 4 + kg], 11 * CB)
                wdv = wview(wds, 11)
                for dc in range(2):
                    for k in range(11):
                        mm(pds[dc][0][:, :], wdv[:, k, dc * 128:(dc + 1) * 128], aT.ap[:, kg * 11 + k, :],
                           kg == 0 and k == 0, kg == 3 and k == 10,
                           [wds.bufs[0], aT.bufs[kg * 11 + k]], [pds[dc][1]], signal=(k == 10))
            for dc in range(2):
                c = nb * 2 + dc
                stt(xT.ap[:, c, :], pds[dc][0][:, :], 0.5, xsrc.ap[:, c, :], ALU.mult, ALU.add,
                    [pds[dc][1], xsrc.bufs[c]], [xT.bufs[c]])
                acc.push(c)
        acc.finish()
        A.release(mark)

    def proj_feature_major_gen(w_d, blk0, nblk, rhs, evac, interleave_first=False):
        for nb in range(nblk):
            sl = wget(w_d[blk0 + nb], SLOT)
            wv = wview(sl, 16)
            if interleave_first and nb == 0:
                banks = [ps_next() for _ in range(2)]
                for kc in range(DC):
                    for dc in range(2):
                        mm(banks[dc][0][:, :], wv[:, kc, dc * 128:(dc + 1) * 128], rhs.ap[:, kc, :], kc == 0, kc == DC - 1,
                           [sl.bufs[0], rhs.bufs[kc]], [banks[dc][1]])
                for dc in range(2):
                    evac(nb * 2 + dc, banks[dc][0], banks[dc][1])
                    yield
                continue
            for dc in range(2):
                pt, pb = ps_next()
                for kc in range(DC):
                    mm(pt[:, :], wv[:, kc, dc * 128:(dc + 1) * 128], rhs.ap[:, kc, :], kc == 0, kc == DC - 1,
                       [sl.bufs[0], rhs.bufs[kc]], [pb])
                evac(nb * 2 + dc, pt, pb)
                yield

    def proj_feature_major(w_d, blk0, nblk, rhs, evac, interleave_first=False):
        for _ in proj_feature_major_gen(w_d, blk0, nblk, rhs, evac, interleave_first):
            pass

    def accum_into_x(w_d, rhs):
        acc = StatAcc(xT)

        def ev(ch, pt, pb):
            tten(xT.ap[:, ch, :], pt[:, :], xT.ap[:, ch, :], ALU.add, [pb, xT.bufs[ch]], [xT.bufs[ch]])
            acc.push(ch)
        for i, _ in enumerate(proj_feature_major_gen(w_d, 0, 8, rhs, ev)):
            if i % 2 == 1 and len(acc.pending) > 2:
                keep = acc.pending[-2:]
                acc.pending = acc.pending[:-2]
                acc.flush()
                acc.pending = keep
        acc.finish()

    def proj_token_major(w_d, blk0, nblk, evac, interleave_first=False):
        for nb in range(nblk):
            sl = wget(w_d[blk0 + nb], SLOT)
            wv = wview(sl, 16)
            if interleave_first and nb == 0:
                banks = [ps_next() for _ in range(4)]
                for kc in range(DC):
                    for tb in range(4):
                        mm(banks[tb][0][:, :CB], hT.ap[:, kc, tb * 128:(tb + 1) * 128], wv[:, kc, :], kc == 0, kc == DC - 1,
                           [sl.bufs[0], hT.bufs[kc]], [banks[tb][1]])
                for tb in range(4):
                    evac(nb, tb, banks[tb][0], banks[tb][1])
                continue
            for tb in range(4):
                pt, pb = ps_next()
                for kc in range(DC):
                    mm(pt[:, :CB], hT.ap[:, kc, tb * 128:(tb + 1) * 128], wv[:, kc, :], kc == 0, kc == DC - 1,
                       [sl.bufs[0], hT.bufs[kc]], [pb])
                evac(nb, tb, pt, pb)

    def gla_gates(mx):
        a_aug, l_t, expD, gam = mx["a_aug"], mx["l"], mx["expD"], mx["gam"]
        pt, pb = ps_next()
        for kc in range(DC):
            mm(pt[:, :], wa_bf.ap[:, kc, :], hT.ap[:, kc, :], kc == 0, kc == DC - 1,
               [wa_bf.bufs[0], hT.bufs[kc]], [pb])
        vmemset(a_aug.ap[0:32, :], 1.0, a_aug.bufs)
        actf(a_aug.ap[0:16, :], pt[0:16, :], AF.Copy, [pb], a_aug.bufs)
        pg, pgb = ps_next(pin=True)
        for tb in range(4):
            pz, pzb = ps_next()
            mm(pz[:, :], a_aug.ap[0:17, tb * 128:(tb + 1) * 128], cap[0:17, C_WA2:C_WA2 + 512], True, True,
               [a_aug.bufs[0], cst.bufs[0]], [pzb])
            actf(l_t.ap, pz[:, :], AF.Exp, [pzb], l_t.bufs, scale=-1.0)
            actf(l_t.ap, l_t.ap, AF.Ln, l_t.bufs, l_t.bufs, bias=1.0)
            pd, pdb = ps_next()
            mm(pd[:, :], cap[:, C_MTRI:C_MTRI + 128], l_t.ap, True, True, [cst.bufs[0], l_t.bufs[0]], [pdb])
            actf(expD.ap[:, tb, :], pd[:, :], AF.Exp, [pdb], [expD.bufs[tb]])
            for h in range(4):
                col = (tb * 4 + h) * 2
                mm(pg[:, col:col + 2], l_t.ap[:, h * 128:(h + 1) * 128], cap[:, C_IND:C_IND + 2], True, True,
                   [l_t.bufs[0], cst.bufs[0]], [pgb], signal=True)
        actf(gam.ap, pg[:, 0:32], AF.Exp, [pgb], gam.bufs)
        ps_unpin(pgb)

    def gla_kv_proj(mx, with_g, parts=("k", "v", "g")):
        expD, kdec, v = mx["expD"], mx["kdec"], mx["v"]

        def ev_k(nb, tb, pt, pb):
            tten(kdec.ap[:, tb, nb * CB:(nb + 1) * CB], pt[:, :CB], expD.ap[:, tb, nb * CB:(nb + 1) * CB], ALU.mult,
                 [pb, expD.bufs[tb]], [kdec.bufs[tb]])
        if "k" in parts:
            proj_token_major(win_d, 6, 2, ev_k)

        def ev_v(nb, tb, pt, pb):
            actf(v.ap[:, tb, nb * CB:(nb + 1) * CB], pt[:, :CB], AF.Copy, [pb], [v.bufs[tb]])
        if "v" in parts:
            proj_token_major(win_d, 8, 4, ev_v, interleave_first=True)
        if with_g and "g" in parts:
            G2, gt = mx["G2"], mx["gtmp"]

            def ev_g(nb, tb, pt, pb):
                g = gt[(nb * 4 + tb) % 2]
                actf(g.ap, pt[:, :CB], AF.Silu, [pb], g.bufs)
                tten(G2.ap[:, tb, nb * CB:(nb + 1) * CB], g.ap, cap[:, C_HNORM + nb * CB:C_HNORM + (nb + 1) * CB], ALU.mult,
                     [g.bufs[0], cst.bufs[0]], [G2.bufs[tb]])
            proj_token_major(win_d, 12, 4, ev_g)

    def gla_recur(mx, S2, tile_idx, full, filler=None):
        kdec, v, gam = mx["kdec"], mx["v"], mx["gam"]
        gam4 = gam.ap.rearrange("p (t h c) -> p t h c", h=4, c=2)

        def step():
            if filler is not None:
                next(filler, None)

        def stage_a(c):
            tb, half = c // 2, c % 2
            rows = slice(half * 64, half * 64 + 64)
            Sin, Sout = S2[c % 2], S2[(c + 1) % 2]
            banks = [ps_next(), ps_next()]
            for h in range(4):
                pt, pb = banks[h // 2]
                mm(pt[:, (h % 2) * 256:(h % 2) * 256 + 256], kdec.ap[rows, tb, h * 128:(h + 1) * 128],
                   v.ap[rows, tb, h * 256:(h + 1) * 256], True, True, [kdec.bufs[tb], v.bufs[tb]], [pb], signal=True)
            for h in range(4):
                pt, pb = banks[h // 2]
                stt(Sout.ap[:, h, :], Sin.ap[:, h, :], gam4[:, tb, h, half:half + 1], pt[:, (h % 2) * 256:(h % 2) * 256 + 256],
                    ALU.mult, ALU.add, [Sin.bufs[h], gam.bufs[0], pb], [Sout.bufs[h]])
            if full:
                Sbf = mx["Sbf"][c % 2]
                for h in range(4):
                    actf(Sbf.ap[:, h, :], Sout.ap[:, h, :], AF.Copy, [Sout.bufs[h]], [Sbf.bufs[h]])

        def stage_b(c):
            tb, half = c // 2, c % 2
            rows = slice(half * 64, half * 64 + 64)
            Sbf = mx["Sbf"][c % 2]
            qT, G2, y, ss, rs, junk, mixT = mx["qT"], mx["G2"], mx["y"], mx["ss"], mx["rs"], mx["junk"], mx["mixT"]
            yb, ssb, rsb, jb = [y.bufs[half]], [ss.bufs[half]], [rs.bufs[half]], [junk.bufs[half]]
            obanks = [ps_next(), ps_next()]
            for h in range(4):
                pt, pb = obanks[h // 2]
                mm(pt[:, (h % 2) * 256:(h % 2) * 256 + 256], qT.ap[:, h, tb * 128:(tb + 1) * 128], Sbf.ap[:, h, :],
                   True, True, [qT.bufs[0], Sbf.bufs[h]], [pb], signal=True)
            for h in range(4):
                pt, pb = obanks[h // 2]
                actf(junk.ap[rows, :], pt[rows, (h % 2) * 256:(h % 2) * 256 + 256], AF.Square, [pb], jb + ssb,
                     accum=ss.ap[rows, h:h + 1])
            actf(rs.ap[rows, :], ss.ap[rows, :], AF.Sqrt, ssb, rsb, scale=1.0 / 256, bias=EPS)
            dve.op(lambda e, o=rs.ap[rows, :]: e.reciprocal(out=o, in_=o), rsb, rsb)
            for h in range(4):
                pt, pb = obanks[h // 2]
                stt(y.ap[rows, h * 256:(h + 1) * 256], pt[rows, (h % 2) * 256:(h % 2) * 256 + 256], rs.ap[rows, h:h + 1],
                    G2.ap[rows, tb, h * 256:(h + 1) * 256], ALU.mult, ALU.mult, [pb, rsb[0], G2.bufs[tb]], yb)

        def stage_c(c):
            tb, half = c // 2, c % 2
            rows = slice(half * 64, half * 64 + 64)
            y, mixT = mx["y"], mx["mixT"]
            yb = [y.bufs[half]]
            ptt, ptb = ps_next()
            for fcx in range(8):
                tr(ptt[:, fcx * 64:(fcx + 1) * 64], y.ap[rows, fcx * 128:(fcx + 1) * 128], cap[rows, C_IDENT + half * 64:C_IDENT + half * 64 + 64],
                   [yb[0], cst.bufs[0]], [ptb], signal=(fcx == 7))
            actf(mixT.ap[:, 8:16, c * 64:(c + 1) * 64], ptt[:, :].rearrange("p (a b) -> p a b", b=64), AF.Copy,
                 [ptb], mixT.bufs[8:16])

        if not full:
            for c in range(8):
                stage_a(c)
                step()
            return
        stage_a(0)
        stage_a(1)
        stage_b(0)
        for c in range(8):
            if c + 2 < 8:
                stage_a(c + 2)
            if c + 1 < 8:
                stage_b(c + 1)
            step()
            stage_c(c)

    def pool_mixer(mx, tile_idx):
        u_ext, sA, sB, dd, mixT = mx["u_ext"], mx["sA"], mx["sB"], mx["d"], mx["mixT"]
        state = {}

        def ev_u(cc, pt, pb):
            g = cc // 2
            w = 2 ** (g + 1)
            ue = u_ext
            actf(ue.ap[:, 0:16], halo.ap[:, cc, :], AF.Copy, halo.bufs, ue.bufs)
            actf(ue.ap[:, 16:528], pt[:, :], AF.Copy, [pb], ue.bufs)
            actf(halo.ap[:, cc, :], ue.ap[:, 512:528], AF.Copy, ue.bufs, halo.bufs)
            cur, lo = ue, 0
            nxt = [sA, sB]
            for lev in range(g + 1):
                sh = 2 ** lev
                o = nxt[lev % 2]
                tten(o.ap[:, lo + sh:528], cur.ap[:, lo + sh:528], cur.ap[:, lo:528 - sh], ALU.add,
                     cur.bufs, o.bufs)
                cur, lo = o, lo + sh
            dslot = dd[g % 2]
            stt(dslot.ap[:, cc % 2, :], cur.ap[:, 16:528], 1.0 / w, ue.ap[:, 16:528], ALU.mult, ALU.subtract,
                cur.bufs + ue.bufs, dslot.bufs)
            if tile_idx == 0:
                tmp = mx["ptmp"]
                tten(tmp.ap, cur.ap[:, 16:32], cap[:, C_PTAB + g * 16:C_PTAB + (g + 1) * 16], ALU.mult,
                     cur.bufs + cst.bufs, tmp.bufs)
                tten(dslot.ap[:, cc % 2, 0:16], tmp.ap, ue.ap[:, 16:32], ALU.subtract, tmp.bufs + ue.bufs, dslot.bufs)
            if cc % 2 == 1:
                for oc in range(2):
                    pt2, pb2 = ps_next()
                    for kc in range(2):
                        mm(pt2[:, :], poolw.ap[:, g * 2 + kc, oc * 128:(oc + 1) * 128], dslot.ap[:, kc, :], kc == 0, kc == 1,
                           [poolw.bufs[0], dslot.bufs[0]], [pb2])
                    ch = g * 2 + oc
                    actf(mixT.ap[:, ch, :], pt2[:, :], AF.Copy, [pb2, cst.bufs[0]], [mixT.bufs[ch]],
                         scale=cap[:, C_PSCALE + ch:C_PSCALE + ch + 1])
        yield from proj_feature_major_gen(win_d, 0, 4, hT, ev_u)

    def xattn():
        mark = A.mark()
        qx = alloc([128, DC, T], BF16, nbufs=DC)
        p2 = [alloc([128, 4, 256], F32) for _ in range(2)]
        pT = alloc([128, 2, 4, T], BF16)
        oT = alloc([128, DC, T], BF16, nbufs=DC)
        mxt2 = [alloc([128, 4], F32) for _ in range(2)]
        nmx2 = [alloc([128, 4], F32) for _ in range(2)]
        sm2 = [alloc([128, 4], F32) for _ in range(2)]
        rsm2 = [alloc([128, 4], F32) for _ in range(2)]
        norm_to(xT, 2, hT, T, stats=rstd)

        def ev_q(ch, pt, pb):
            actf(qx.ap[:, ch, :], pt[:, :], AF.Copy, [pb], [qx.bufs[ch]], scale=float(512 ** -0.5))
        proj_feature_major(wq_d, 0, 8, hT, ev_q, interleave_first=True)

        lbanks = {}

        def emit_logits(tb):
            banks = [ps_next(), ps_next()]
            lbanks[tb] = banks
            for h in range(4):
                pt, pb = banks[h // 2]
                for dc in range(4):
                    mm(pt[:, (h % 2) * 256:(h % 2) * 256 + 256], qx.ap[:, h * 4 + dc, tb * 128:(tb + 1) * 128],
                       KT.ap[:, h * 4 + dc, :], dc == 0, dc == 3, [qx.bufs[h * 4 + dc], KT.bufs[0]], [pb])

        def emit_softmax(tb):
            banks = lbanks.pop(tb)
            p, mxt, nmx, sm, rsm = p2[tb % 2], mxt2[tb % 2], nmx2[tb % 2], sm2[tb % 2], rsm2[tb % 2]
            for bk in range(2):
                pt, pb = banks[bk]
                dve.op(lambda e, o=mxt.ap[:, bk * 2:bk * 2 + 2], i=pt[:, :].rearrange("p (a b) -> p a b", b=256):
                       e.tensor_reduce(out=o, in_=i, axis=AX.X, op=ALU.max), [pb], mxt.bufs)
            ts(nmx.ap, mxt.ap, -1.0, None, ALU.mult, None, mxt.bufs, nmx.bufs)
            for h in range(4):
                pt, pb = banks[h // 2]
                actf(p.ap[:, h, :], pt[:, (h % 2) * 256:(h % 2) * 256 + 256], AF.Exp, [pb, nmx.bufs[0]], p.bufs,
                     bias=nmx.ap[:, h:h + 1], accum=sm.ap[:, h:h + 1])
                sm.bufs[0].w = dict(p.bufs[0].w)
            dve.op(lambda e, o=rsm.ap, i=sm.ap: e.reciprocal(out=o, in_=i), sm.bufs + p.bufs, rsm.bufs)
            for h in range(4):
                ts(p.ap[:, h, :], p.ap[:, h, :], rsm.ap[:, h:h + 1], None, ALU.mult, None, p.bufs + rsm.bufs, p.bufs)
            tbanks = [ps_next(), ps_next()]
            for mc in range(2):
                pt, pb = tbanks[mc]
                for h in range(4):
                    tr(pt[:, h * 128:(h + 1) * 128], p.ap[:, h, mc * 128:(mc + 1) * 128], ident, [p.bufs[0], cst.bufs[0]], [pb],
                       signal=(h == 3))
                actf(pT.ap[:, mc, :, tb * 128:(tb + 1) * 128], pt[:, :].rearrange("p (a b) -> p a b", b=128), AF.Copy,
                     [pb], pT.bufs)

        emit_logits(0)
        for tb in range(4):
            if tb + 1 < 4:
                emit_logits(tb + 1)
            emit_softmax(tb)
        for ch in range(DC):
            h, dc = ch // 4, ch % 4
            pt, pb = ps_next()
            for mc in range(2):
                mm(pt[:, :], Vt.ap[:, mc, h * 512 + dc * 128:h * 512 + (dc + 1) * 128], pT.ap[:, mc, h, :], mc == 0, mc == 1,
                   [Vt.bufs[0], pT.bufs[0]], [pb])
            actf(oT.ap[:, ch, :], pt[:, :], AF.Copy, [pb], [oT.bufs[ch]])

        accum_into_x(wo_d, oT)
        A.release(mark)

    sp.op(lambda e: e.dma_start(out=cst.ap, in_=cst_d), (), cst.bufs, dma_sem=misc_sem)
    vmemset(ones_bf.ap, 1.0, ones_bf.bufs)
    vmemset(wa_bf.ap.rearrange("p a b -> p (a b)"), 0.0, wa_bf.bufs)
    pool.op(lambda e: e.dma_start(out=wa_bf.ap[:, :, 0:16], in_=wina_d.rearrange("p (a b) -> p a b", b=16)), (), wa_bf.bufs, dma_sem=cc_sem)

    def kv_norm(mhT):
        xm = TT(xT.ap[:, :, 0:256], xT.bufs)
        sp_dma(xm.ap, memT_d, (), xT.bufs)
        norm_to(xm, 5, mhT, 256)

    def k_gen(mhT):
        for nb in range(8):
            sl = wget(wkv_d[nb], SLOT)
            wv = wview(sl, 16)
            for dc in range(2):
                pt, pb = ps_next()
                for kc in range(DC):
                    mm(pt[:, :256], wv[:, kc, dc * 128:(dc + 1) * 128], mhT.ap[:, kc, :], kc == 0, kc == DC - 1,
                       [sl.bufs[0], mhT.bufs[kc]], [pb])
                actf(KT.ap[:, nb * 2 + dc, :], pt[:, :256], AF.Copy, [pb], KT.bufs)
                yield

    def v_gen(mhT):
        for nb in range(8):
            sl = wget(wkv_d[8 + nb], SLOT)
            wv = wview(sl, 16)
            for mc in range(2):
                pt, pb = ps_next()
                for kc in range(DC):
                    mm(pt[:, :CB], mhT.ap[:, kc, mc * 128:(mc + 1) * 128], wv[:, kc, :], kc == 0, kc == DC - 1,
                       [sl.bufs[0], mhT.bufs[kc]], [pb])
                actf(Vt.ap[:, mc, nb * CB:(nb + 1) * CB], pt[:, :CB], AF.Copy, [pb], Vt.bufs)
                yield

    vmemset(Sst.ap, 0.0, Sst.bufs)
    vmemset(halo.ap, 0.0, halo.bufs)
    NPRE = 3 * NT
    xn_mark = A.mark()
    xnext = alloc([128, DC, T], F32, nbufs=DC)
    sp_dma(xnext.ap, xT_d[:, :, 0:T], (), xnext.bufs)
    have_n = False
    mhT = None
    for t in range(NPRE):
        norm_to(xnext, 0, hT, T, stats=rstd if have_n else None)
        ffn(0, xsrc=xnext)
        sp_dma(xnext.ap, xT_d[:, :, (t + 1) * T:(t + 2) * T], (), xnext.bufs)
        norm_to(xT, 1, hT, T, stats=rstd)
        filler = None
        if t == NPRE - 2:
            mh_mark = A.mark()
            mhT = alloc([128, DC, 256], BF16, nbufs=DC)
            activate_top(KT, TOP_KT, 2048)
            filler = k_gen(mhT)
        elif t == NPRE - 1:
            activate_top(Vt, TOP_VT, 2048)
            activate_top(poolw, TOP_PW, 1024)
            pool.op(lambda e: e.dma_start(out=poolw.ap.rearrange("p a b -> p (a b)"), in_=poolw_d), (), poolw.bufs, dma_sem=cc_sem)
            filler = v_gen(mhT)
        mark = A.mark()
        mx = {
            "a_aug": alloc([32, T], F32), "l": alloc([128, 512], F32),
            "expD": alloc([128, 4, 512], F32, nbufs=4), "gam": alloc([128, 32], F32),
            "kdec": alloc([128, 4, 512], BF16, nbufs=4), "v": alloc([128, 4, 1024], BF16, nbufs=4),
        }
        gla_kv_proj(mx, with_g=False, parts=("v",))
        gla_gates(mx)
        gla_kv_proj(mx, with_g=False, parts=("k",))
        if t == NPRE - 2:
            kv_norm(mhT)
        gla_recur(mx, [Sst, Sst2], t, full=False, filler=filler)
        if filler is not None:
            for _ in filler:
                pass
        if t == NPRE - 1:
            for nb in range(4):
                sl = wget(win_d[nb], SLOT)
                wv = wview(sl, 16)
                for dc in range(2):
                    pt, pb = ps_next()
                    for kc in range(DC):
                        mm(pt[:, 0:16], wv[:, kc, dc * 128:(dc + 1) * 128], hT.ap[:, kc, T - 16:T], kc == 0, kc == DC - 1,
                           [sl.bufs[0], hT.bufs[kc]], [pb])
                    cc = nb * 2 + dc
                    actf(halo.ap[:, cc, :], pt[:, 0:16], AF.Copy, [pb], halo.bufs)
        A.release(mark)
        if t == NPRE - 1:
            A.release(mh_mark)
        accn = StatAcc(xnext, T, rstd)
        for c in range(DC):
            accn.push(c)
        accn.finish()
        have_n = True

    rstd_nx = TT(Sst2.ap.rearrange("p h v -> p (h v)")[:, 0:T], Sst2.bufs[0:2])
    for t in range(NT):
        tsl = slice(t * T, (t + 1) * T)
        norm_to(xnext, 0, hT, T, stats=rstd if t == 0 else rstd_nx)
        ffn(0, xsrc=xnext)
        A.release(xn_mark)
        if DEBUG_STAGE == "x1":
            sp_dma(outT_d[:, :, tsl], xT.ap, xT.bufs, [Buf()])
            continue
        norm_to(xT, 1, hT, T, stats=rstd)
        mark = A.mark()
        mx = {
            "a_aug": alloc([32, T], F32), "l": alloc([128, 512], F32),
            "expD": alloc([128, 4, 512], F32, nbufs=4), "gam": alloc([128, 32], F32),
            "kdec": alloc([128, 4, 512], BF16, nbufs=4), "v": alloc([128, 4, 1024], BF16, nbufs=4),
            "G2": alloc([128, 4, 1024], BF16, nbufs=4), "gtmp": [alloc([128, CB], F32) for _ in range(2)],
            "mixT": alloc([128, DC, T], BF16, nbufs=DC), "qT": alloc([128, 4, T], BF16),
            "u_ext": alloc([128, 528], F32), "sA": alloc([128, 528], F32), "sB": alloc([128, 528], F32),
            "d": [alloc([128, 2, T], BF16) for _ in range(2)], "ptmp": alloc([128, 16], F32),
            "y": halves(alloc([128, 1024], F32)), "Sbf": [alloc([128, 4, 256], BF16, nbufs=4) for _ in range(2)],
            "junk": halves(alloc([128, 256], BF16)), "ss": halves(alloc([128, 4], F32)), "rs": halves(alloc([128, 4], F32)),
        }
        gla_kv_proj(mx, with_g=True, parts=("v",))
        gla_gates(mx)
        qT = mx["qT"]

        def ev_qg(ch, pt, pb, qT=qT):
            actf(qT.ap[:, ch, :], pt[:, :], AF.Copy, [pb], qT.bufs, scale=float(128 ** -0.5))
        proj_feature_major(win_d, 4, 2, hT, ev_qg)
        gla_kv_proj(mx, with_g=True, parts=("k", "g"))
        filler = pool_mixer(mx, t)
        gla_recur(mx, [Sst, Sst2], t, full=True, filler=filler)
        for _ in filler:
            pass
        mixT = mx["mixT"]

        accum_into_x(wout_d, mixT)
        A.release(mark)
        if DEBUG_STAGE == "x2":
            sp_dma(outT_d[:, :, tsl], xT.ap, xT.bufs, [Buf()])
            continue
        xattn()
        if DEBUG_STAGE == "x3":
            sp_dma(outT_d[:, :, tsl], xT.ap, xT.bufs, [Buf()])
            continue
        if t + 1 < NT:
            xn_mark = A.mark()
            xnext = alloc([128, DC, T], F32, nbufs=DC)
            sp_dma(xnext.ap, xT_d[:, :, (NPRE + t + 1) * T:(NPRE + t + 2) * T], (), xnext.bufs)
        norm_to(xT, 3, hT, T, stats=rstd)
        hook = None
        if t + 1 < NT:
            def hook(xn=xnext):
                accn2 = StatAcc(xn, T, rstd_nx)
                for c in range(DC):
                    accn2.push(c)
                accn2.finish()
        ffn(1, mid_hook=hook)
        mark = A.mark()
        outb = alloc([128, DC, T], F32, nbufs=DC)
        for c in range(DC):
            stt(outb.ap[:, c, :], xT.ap[:, c, :], gain_ap(4, c), rstd.ap[:, :T], ALU.mult, ALU.mult,
                [xT.bufs[c], rstd.bufs[0], cst.bufs[0]], [outb.bufs[c]])
            if c % 4 == 3:
                sp_dma(outT_d[:, c - 3:c + 1, tsl], outb.ap[:, c - 3:c + 1, :], outb.bufs[c - 3:c + 1], [Buf()])
        A.release(mark)

    for s in sp_sems:
        if s.n:
            sp._wait(s, s.n)

    block = es.enter_context(nc.Block())

    @block.tensor
    def _(e):
        pe.replay(e)

    @block.scalar
    def _(e):
        act.replay(e)

    @block.vector
    def _(e):
        dve.replay(e)

    @block.gpsimd
    def _(e):
        pool.replay(e)

    @block.sync
    def _(e):
        sp.replay(e)

    es.close()
    return nc


def _tile_w(W, cb, kgroups=None):
    K, N = W.shape
    KC = K // 128
    if kgroups is None:
        kgroups = [KC]
    Wr = W.reshape(KC, 128, N // cb, cb)
    blocks = []
    for nb in range(N // cb):
        k0 = 0
        for kn in kgroups:
            blk = Wr[k0:k0 + kn, :, nb, :]
            blocks.append(np.ascontiguousarray(blk.transpose(1, 0, 2)).reshape(128, kn * cb))
            k0 += kn
    return np.ascontiguousarray(np.stack(blocks))


def _fm(v, nch):
    return np.ascontiguousarray(np.asarray(v, np.float32).reshape(nch, 128).T)


_PROGRAM = None


def kernel(x, mem, ffn1_norm, ffn1_w_gate, ffn1_w_up, ffn1_w_down, mix_norm, w_in,
           pool_w, pool_scale, gla_w_a2, gla_b_a, gla_head_norm, w_out,
           xattn_norm, mem_norm, xattn_w_q, xattn_w_kv, xattn_w_o,
           ffn2_norm, ffn2_w_gate, ffn2_w_up, ffn2_w_down, final_norm):
    global _PROGRAM
    f = lambda a: np.asarray(a, np.float32)
    x, mem = f(x), f(mem)
    shared = {
        "wg1": _tile_w(f(ffn1_w_gate)[0], CB), "wu1": _tile_w(f(ffn1_w_up)[0], CB),
        "wd1": _tile_w(f(ffn1_w_down)[0], CB, [11] * 4),
        "wg2": _tile_w(f(ffn2_w_gate)[0], CB), "wu2": _tile_w(f(ffn2_w_up)[0], CB),
        "wd2": _tile_w(f(ffn2_w_down)[0], CB, [11] * 4),
        "win": _tile_w(f(w_in)[0][:, :4096], CB),
        "wina": np.ascontiguousarray(f(w_in)[0][:, 4096:4112].reshape(16, 128, 16).transpose(1, 0, 2)).reshape(128, 256),
        "poolw": np.ascontiguousarray(f(pool_w)[0].reshape(4, 2, 128, 256).transpose(2, 0, 1, 3)).reshape(128, 2048),
        "wout": _tile_w(f(w_out)[0], CB), "wq": _tile_w(f(xattn_w_q)[0], CB),
        "wkv": _tile_w(f(xattn_w_kv)[0], CB), "wo": _tile_w(f(xattn_w_o)[0], CB),
    }
    cst = np.zeros((128, NCST), np.float32)
    gains = [ffn1_norm[0], mix_norm[0], xattn_norm[0], ffn2_norm[0], final_norm, mem_norm[0]]
    for i, g in enumerate(gains):
        cst[:, C_GAIN + i * 16:C_GAIN + (i + 1) * 16] = _fm(g, 16)
    cst[:, C_PSCALE:C_PSCALE + 8] = _fm(f(pool_scale)[0], 8)
    cst[:, C_HNORM:C_HNORM + 1024] = f(gla_head_norm)[0][None, :]
    cst[0:16, C_WA2:C_WA2 + 512] = f(gla_w_a2)[0]
    cst[16, C_WA2:C_WA2 + 512] = f(gla_b_a)[0]
    ii = np.arange(128)
    same = (ii[:, None] // 64) == (ii[None, :] // 64)
    cst[:, C_MTRI:C_MTRI + 128] = np.where(same & (ii[:, None] > ii[None, :]), -1.0 / 16.0, 0.0)
    cst[:, C_IND:C_IND + 2] = np.where((ii[:, None] // 64) == np.arange(2)[None, :], -1.0 / 16.0, 0.0)
    cst[:, C_IDENT:C_IDENT + 128] = np.eye(128, dtype=np.float32)

    in_maps = []
    for core in range(NCORES):
        b, q = core // 4, core % 4
        xs = np.zeros((4 * TOK, D), np.float32)
        xs[(3 - q) * TOK:] = x[b, 0:(q + 1) * TOK, :]
        xTc = np.ascontiguousarray(xs.reshape(4 * TOK, DC, 128).transpose(2, 1, 0))
        memTc = np.ascontiguousarray(mem[b].reshape(256, DC, 128).transpose(2, 1, 0))
        c = cst.copy()
        for r in range(NCORES):
            m = 1.0 if (r // 4 == b and r % 4 < q) else 0.0
            c[:, C_M + r] = m
            c[:, C_ONEM + r] = 1.0 - m
            c[:, C_HM + r] = 1.0 if (r == core - 1 and q > 0) else 0.0
        for g in range(4):
            w = 2 ** (g + 1)
            tt = np.arange(16)
            c[:, C_PTAB + g * 16:C_PTAB + (g + 1) * 16] = (1.0 / np.minimum(tt + 1, w) if q == 0 else np.full(16, 1.0 / w))[None, :]
        d = {"xT": xTc, "memT": memTc, "cst": c}
        d.update(shared)
        in_maps.append(d)

    if _PROGRAM is None:
        _PROGRAM = build_program()
    res = run_bass_kernel_spmd(_PROGRAM, in_maps, core_ids=list(range(NCORES)))
    out = np.empty((2, 8192, D), np.float32)
    for core in range(NCORES):
        b, q = core // 4, core % 4
        oT = np.asarray(res.results[core]["outT"])
        out[b, q * TOK:(q + 1) * TOK, :] = oT.transpose(2, 1, 0).reshape(TOK, D)
    return out
```

```python
import numpy as np
from contextlib import ExitStack
import concourse.bass as bass
import concourse.mybir as mybir
from concourse.bass_utils import run_bass_kernel_spmd

F32 = mybir.dt.float32
BF16 = mybir.dt.bfloat16
AF = mybir.ActivationFunctionType
ALU = mybir.AluOpType
AX = mybir.AxisListType

NCORES = 8
TOK = 2048
T = 512
NT = TOK // T
D = 2048
DC = 16
FF = 5632
FCH = 44
EPS = 1e-6
CB = 256
SLOT = 16 * CB
NSLOT = 4
PAYW = 1024 + 4 + 128

C_GAIN = 0
C_PSCALE = 96
C_HNORM = 104
C_WA2 = 1128
C_MTRI = 1640
C_IND = 1768
C_IDENT = 1770
C_M = 1898
C_ONEM = 1906
C_HM = 1914
C_PTAB = 1922
NCST = 1986

DEBUG_STAGE = None


class Sem:
    def __init__(self, h):
        self.h = h
        self.n = 0


class Buf:
    __slots__ = ("w", "r")

    def __init__(self, init=None):
        self.w = dict(init) if init else {}
        self.r = {}


def _merge(dst, src):
    for s, c in src.items():
        if dst.get(s, 0) < c:
            dst[s] = c


class Queue:
    def __init__(self, name, sem):
        self.name = name
        self.sem = sem
        self.prog = []
        self.waited = {}
        self.is_pe = name == "pe"

    def _wait(self, s, c):
        if s is self.sem and self.is_pe:
            return
        if self.waited.get(s, 0) >= c:
            return
        self.waited[s] = c
        self.prog.append(("wait", s, c))

    def op(self, fn, reads=(), writes=(), signal=True, dma_sem=None):
        for b in reads:
            for s, c in b.w.items():
                self._wait(s, c)
        for b in writes:
            for s, c in b.w.items():
                self._wait(s, c)
            for s, c in b.r.items():
                self._wait(s, c)
        if dma_sem is not None:
            if dma_sem.n:
                self._wait(dma_sem, dma_sem.n)
            dma_sem.n += 16
            ev = (dma_sem, dma_sem.n)
            self.prog.append(("dma", fn, dma_sem))
        elif signal:
            self.sem.n += 1
            ev = (self.sem, self.sem.n)
            self.prog.append(("op", fn, True))
        else:
            ev = (self.sem, self.sem.n + 1)
            self.prog.append(("op", fn, False))
        for b in reads:
            if b.r.get(ev[0], 0) < ev[1]:
                b.r[ev[0]] = ev[1]
        for b in writes:
            b.w = {ev[0]: ev[1]}
            b.r = {}
        return ev

    def replay(self, eng):
        for it in self.prog:
            if it[0] == "wait":
                eng.wait_ge(it[1].h, it[2])
            elif it[0] == "op":
                ins = it[1](eng)
                if it[2]:
                    ins.then_inc(self.sem.h, 1)
            else:
                ins = it[1](eng)
                ins.then_inc(it[2].h, 16)


class Arena:
    def __init__(self, cap):
        self.cap = cap
        self.top = 0
        self.dead = []
        self.live = []

    def alloc(self, nf32):
        nf32 = (nf32 + 15) // 16 * 16
        lo = self.top
        self.top += nf32
        assert self.top <= self.cap, ("SBUF arena overflow", self.top, self.cap)
        return lo

    def newbuf(self, lo, hi, track=True):
        init = {}
        for (a, b, ev) in self.dead:
            if a < hi and lo < b:
                _merge(init, ev)
        bf = Buf(init)
        if track:
            self.live.append((lo, hi, bf))
        return bf

    def mark(self):
        return self.top

    def release(self, mark):
        keep = []
        for (a, b, bf) in self.live:
            if a >= mark:
                ev = {}
                _merge(ev, bf.w)
                _merge(ev, bf.r)
                self.dead.append((a, b, ev))
            else:
                keep.append((a, b, bf))
        self.live = keep
        self.dead = [d for d in self.dead if d[0] < self.cap]
        self.top = mark


class TT:
    def __init__(self, ap, bufs):
        self.ap = ap
        self.bufs = bufs


def build_program():
    nc = bass.Bass("TRN2", target_bir_lowering=False)

    def din(name, shape):
        return nc.dram_tensor(name, list(shape), F32, kind="ExternalInput").ap()

    xT_d = din("xT", [128, DC, 4 * TOK])
    memT_d = din("memT", [128, DC, 256])
    cst_d = din("cst", [128, NCST])
    wg_d = [din("wg1", [22, 128, SLOT]), din("wg2", [22, 128, SLOT])]
    wu_d = [din("wu1", [22, 128, SLOT]), din("wu2", [22, 128, SLOT])]
    wd_d = [din("wd1", [32, 128, 11 * CB]), din("wd2", [32, 128, 11 * CB])]
    win_d = din("win", [16, 128, SLOT])
    wina_d = din("wina", [128, 256])
    poolw_d = din("poolw", [128, 2048])
    wout_d = din("wout", [8, 128, SLOT])
    wq_d = din("wq", [8, 128, SLOT])
    wkv_d = din("wkv", [16, 128, SLOT])
    wo_d = din("wo", [8, 128, SLOT])
    outT_d = nc.dram_tensor("outT", [128, DC, TOK], F32, kind="ExternalOutput").ap()

    ARENA_F32 = 52736
    es = ExitStack()
    arena_t = es.enter_context(nc.sbuf_tensor("arena", [128, ARENA_F32], F32))
    psum_t = [es.enter_context(nc.psum_tensor(f"ps{i}", [128, 512], F32)) for i in range(8)]

    def newsem(name):
        return Sem(es.enter_context(nc.semaphore(name)))

    pe = Queue("pe", newsem("s_pe"))
    act = Queue("act", newsem("s_act"))
    dve = Queue("dve", newsem("s_dve"))
    pool = Queue("pool", newsem("s_pool"))
    sp = Queue("sp", newsem("s_sp"))
    sp_sems = [newsem(f"s_spd{i}") for i in range(6)]
    sp_rr = [0]
    slot_sems = [newsem(f"s_slot{i}") for i in range(NSLOT)]
    cc_sem = newsem("s_cc")
    misc_sem = newsem("s_miscdma")

    A = Arena(ARENA_F32)

    def view(lo, shape, dtype):
        n = int(np.prod(shape[1:]))
        nf = n if dtype == F32 else n // 2
        ap = arena_t[:, lo:lo + nf]
        if dtype != F32:
            ap = ap.bitcast(dtype)
        if len(shape) == 3:
            ap = ap.rearrange("p (a b) -> p a b", b=shape[2])
        elif len(shape) == 4:
            ap = ap.rearrange("p (a b c) -> p a b c", b=shape[2], c=shape[3])
        return ap

    def alloc(shape, dtype, nbufs=1):
        n = int(np.prod(shape[1:]))
        nf = n if dtype == F32 else (n + 1) // 2
        lo = A.alloc(nf)
        ap = view(lo, shape, dtype)
        nfa = (nf + 15) // 16 * 16
        if nbufs == 1:
            bufs = [A.newbuf(lo, lo + nfa)]
        else:
            per = nf // nbufs
            bufs = [A.newbuf(lo + i * per, lo + (i + 1) * per) for i in range(nbufs)]
        return TT(ap, bufs)

    def halves(tt):
        tt.bufs = [tt.bufs[0], Buf(dict(tt.bufs[0].w))]
        return tt

    ps_bufs = [Buf() for _ in range(8)]
    ps_rr = [0]
    ps_pinned = set()

    def ps_next(pin=False):
        while True:
            i = ps_rr[0] % 8
            ps_rr[0] += 1
            if i not in ps_pinned:
                break
        if pin:
            ps_pinned.add(i)
        return psum_t[i], ps_bufs[i]

    def ps_unpin(pb):
        ps_pinned.discard(ps_bufs.index(pb))

    def sp_dma(out, in_, reads, writes):
        s = sp_sems[sp_rr[0] % len(sp_sems)]
        sp_rr[0] += 1
        return sp.op(lambda e, o=out, i=in_: e.dma_start(out=o, in_=i), reads, writes, dma_sem=s)


    cst = alloc([128, NCST], F32)
    ones_bf = alloc([128, 128], BF16)
    wa_bf = alloc([128, 16, 128], BF16)
    TOP_KT = ARENA_F32 - 2048
    TOP_VT = TOP_KT - 2048
    TOP_PW = TOP_VT - 1024
    KT = TT(view(TOP_KT, [128, 16, 256], BF16), None)
    Vt = TT(view(TOP_VT, [128, 2, 2048], BF16), None)
    poolw = TT(view(TOP_PW, [128, 8, 256], BF16), None)

    def activate_top(tt, lo, n):
        assert A.top <= lo, ("top region still in use by the stack", A.top, lo)
        A.cap = min(A.cap, lo)
        tt.bufs = [A.newbuf(lo, lo + n, track=False)]
    xT = alloc([128, DC, T], F32, nbufs=DC)
    hT = alloc([128, DC, T], BF16, nbufs=DC)
    slots = [alloc([128, SLOT], BF16) for _ in range(NSLOT)]
    Sst = alloc([128, 4, 256], F32, nbufs=4)
    Sst2 = alloc([128, 4, 256], F32, nbufs=4)
    halo = alloc([128, 8, 16], F32)
    rstd = alloc([128, T], F32)
    sq = [alloc([128, T], BF16) for _ in range(3)]
    slot_rr = [0]

    cap = cst.ap

    def gain_ap(gidx, c):
        return cap[:, C_GAIN + gidx * 16 + c: C_GAIN + gidx * 16 + c + 1]

    ident = cap[:, C_IDENT:C_IDENT + 128]

    def wget(dram_blk, n):
        i = slot_rr[0] % NSLOT
        slot_rr[0] += 1
        sl = slots[i]
        pool.op(lambda e, o=sl.ap[:, :n], src=dram_blk: e.dma_start(out=o, in_=src),
                reads=(), writes=sl.bufs, dma_sem=slot_sems[i])
        return sl

    def wview(sl, kn, cb=CB):
        return sl.ap[:, :kn * cb].rearrange("p (k c) -> p k c", c=cb)

    def mm(out, lhsT, rhs, start, stop, reads, writes, signal=None):
        if signal is None:
            signal = stop
        return pe.op(lambda e, o=out, l=lhsT, r=rhs, st=start, sp_=stop: e.matmul(o, l, r, start=st, stop=sp_),
                     reads, writes, signal)

    def tr(out, in_, idn, reads, writes, signal=True):
        return pe.op(lambda e, o=out, i=in_, d=idn: e.transpose(o, i, d), reads, writes, signal)

    def actf(out, in_, func, reads, writes, scale=1.0, bias=0.0, accum=None):
        def f(e, o=out, i=in_, fn=func, sc=scale, bi=bias, ac=accum):
            kw = {}
            if ac is not None:
                kw["accum_out"] = ac
            return e.activation(out=o, in_=i, func=fn, scale=sc, bias=bi, **kw)
        return act.op(f, reads, writes)

    def ts(out, in0, s1, s2, op0, op1, reads, writes):
        if op1 is None:
            return dve.op(lambda e, o=out, i=in0, a=s1, p0=op0: e.tensor_scalar(out=o, in0=i, scalar1=a, scalar2=None, op0=p0),
                          reads, writes)
        return dve.op(lambda e, o=out, i=in0, a=s1, b=s2, p0=op0, p1=op1:
                      e.tensor_scalar(out=o, in0=i, scalar1=a, scalar2=b, op0=p0, op1=p1), reads, writes)

    def stt(out, in0, scalar, in1, op0, op1, reads, writes):
        return dve.op(lambda e, o=out, i=in0, s=scalar, j=in1, p0=op0, p1=op1:
                      e.scalar_tensor_tensor(out=o, in0=i, scalar=s, in1=j, op0=p0, op1=p1), reads, writes)

    def tten(out, in0, in1, op, reads, writes):
        return dve.op(lambda e, o=out, i=in0, j=in1, p=op: e.tensor_tensor(out=o, in0=i, in1=j, op=p), reads, writes)

    def vcopy(out, in_, reads, writes):
        return dve.op(lambda e, o=out, i=in_: e.tensor_copy(out=o, in_=i), reads, writes)

    def vmemset(ap, val, writes):
        return dve.op(lambda e, a=ap, v=val: e.memset(a, v), (), writes)

    def rms_stats(src, nch, N, inv_n):
        pt, pb = ps_next()
        for c in range(nch):
            s = sq[c % 3]
            actf(s.ap[:, :N], src.ap[:, c, :N], AF.Square, [src.bufs[c]], s.bufs)
            mm(pt[:, :N], ones_bf.ap, s.ap[:, :N], c == 0, c == nch - 1, [ones_bf.bufs[0], s.bufs[0]], [pb], signal=True)
        actf(rstd.ap[:, :N], pt[:, :N], AF.Ln, [pb], rstd.bufs, scale=inv_n, bias=EPS)
        actf(rstd.ap[:, :N], rstd.ap[:, :N], AF.Exp, rstd.bufs, rstd.bufs, scale=-0.5)

    class StatAcc:
        def __init__(self, src, N=T, dst_rstd=None):
            self.src, self.N = src, N
            self.dst = rstd if dst_rstd is None else dst_rstd
            self.pt, self.pb = ps_next(pin=True)
            self.n = 0
            self.pending = []

        def push(self, c):
            self.pending.append(c)

        def flush(self):
            for c in self.pending:
                s = sq[self.n % 3]
                actf(s.ap[:, :self.N], self.src.ap[:, c, :self.N], AF.Square, [self.src.bufs[c]], s.bufs)
                mm(self.pt[:, :self.N], ones_bf.ap, s.ap[:, :self.N], self.n == 0, self.n == DC - 1,
                   [ones_bf.bufs[0], s.bufs[0]], [self.pb], signal=True)
                self.n += 1
            self.pending = []

        def finish(self):
            self.flush()
            assert self.n == DC
            N = self.N
            actf(self.dst.ap[:, :N], self.pt[:, :N], AF.Ln, [self.pb], self.dst.bufs, scale=1.0 / D, bias=EPS)
            actf(self.dst.ap[:, :N], self.dst.ap[:, :N], AF.Exp, self.dst.bufs, self.dst.bufs, scale=-0.5)
            ps_unpin(self.pb)

    def norm_to(src, gidx, dst, N, stats=None):
        if stats is None:
            rms_stats(src, DC, N, 1.0 / D)
            stats = rstd
        for c in range(DC):
            stt(dst.ap[:, c, :N], src.ap[:, c, :N], gain_ap(gidx, c), stats.ap[:, :N], ALU.mult, ALU.mult,
                [src.bufs[c], cst.bufs[0]] + list(stats.bufs), [dst.bufs[c]])

    def ffn(which, xsrc=None, mid_hook=None):
        xsrc = xT if xsrc is None else xsrc
        mark = A.mark()
        aT = alloc([128, FCH, T], BF16, nbufs=FCH)
        sg = [alloc([128, T], F32) for _ in range(2)]
        for fb in range(22):
            wgs = wget(wg_d[which][fb], SLOT)
            wus = wget(wu_d[which][fb], SLOT)
            wgv, wuv = wview(wgs, 16), wview(wus, 16)
            if fb == 0:
                banks = [ps_next() for _ in range(4)]
                grp = [(wgv, wgs, 0), (wuv, wus, 0), (wgv, wgs, 1), (wuv, wus, 1)]
                for kc in range(DC):
                    for gi, (wv_, sl_, fc) in enumerate(grp):
                        mm(banks[gi][0][:, :], wv_[:, kc, fc * 128:(fc + 1) * 128], hT.ap[:, kc, :], kc == 0, kc == DC - 1,
                           [sl_.bufs[0], hT.bufs[kc]], [banks[gi][1]])
                for fc in range(2):
                    (pg, pgb), (pu, pub) = banks[2 * fc], banks[2 * fc + 1]
                    s_ = sg[fc % 2]
                    actf(s_.ap, pg[:, :], AF.Silu, [pgb], s_.bufs)
                    tten(aT.ap[:, fc, :], s_.ap, pu[:, :], ALU.mult, [s_.bufs[0], pub], [aT.bufs[fc]])
                continue
            for fc in range(2):
                ff = fb * 2 + fc
                pg, pgb = ps_next()
                pu, pub = ps_next()
                for kc in range(DC):
                    mm(pg[:, :], wgv[:, kc, fc * 128:(fc + 1) * 128], hT.ap[:, kc, :], kc == 0, kc == DC - 1,
                       [wgs.bufs[0], hT.bufs[kc]], [pgb])
                for kc in range(DC):
                    mm(pu[:, :], wuv[:, kc, fc * 128:(fc + 1) * 128], hT.ap[:, kc, :], kc == 0, kc == DC - 1,
                       [wus.bufs[0], hT.bufs[kc]], [pub])
                s = sg[ff % 2]
                actf(s.ap, pg[:, :], AF.Silu, [pgb], s.bufs)
                tten(aT.ap[:, ff, :], s.ap, pu[:, :], ALU.mult, [s.bufs[0], pub], [aT.bufs[ff]])
        if mid_hook is not None:
            mid_hook()
        acc = StatAcc(xT)
        for nb in range(8):
            pds = [ps_next() for _ in range(2)]
            for kg in range(4):
                if kg == 1:
                    acc.flush()
                wds = wget(wd_d[which][nb * 4 + kg], 11 * CB)
                wdv = wview(wds, 11)
                for dc in range(2):
                    for k in range(11):
                        mm(pds[dc][0][:, :], wdv[:, k, dc * 128:(dc + 1) * 128], aT.ap[:, kg * 11 + k, :],
                           kg == 0 and k == 0, kg == 3 and k == 10,
                           [wds.bufs[0], aT.bufs[kg * 11 + k]], [pds[dc][1]], signal=(k == 10))
            for dc in range(2):
                c = nb * 2 + dc
                stt(xT.ap[:, c, :], pds[dc][0][:, :], 0.5, xsrc.ap[:, c, :], ALU.mult, ALU.add,
                    [pds[dc][1], xsrc.bufs[c]], [xT.bufs[c]])
                acc.push(c)
        acc.finish()
        A.release(mark)

    def proj_feature_major_gen(w_d, blk0, nblk, rhs, evac, interleave_first=False):
        for nb in range(nblk):
            if interleave_first and nb == 1:
                continue
            sl = wget(w_d[blk0 + nb], SLOT)
            wv = wview(sl, 16)
            if interleave_first and nb == 0:
                sl1 = wget(w_d[blk0 + 1], SLOT)
                wv1 = wview(sl1, 16)
                banks = [ps_next() for _ in range(4)]
                grp = [(wv, sl, 0), (wv, sl, 1), (wv1, sl1, 0), (wv1, sl1, 1)]
                for kc in range(DC):
                    for gi, (wv_, sl_, dc) in enumerate(grp):
                        mm(banks[gi][0][:, :], wv_[:, kc, dc * 128:(dc + 1) * 128], rhs.ap[:, kc, :], kc == 0, kc == DC - 1,
                           [sl_.bufs[0], rhs.bufs[kc]], [banks[gi][1]])
                for gi in range(4):
                    evac(gi, banks[gi][0], banks[gi][1])
                    yield
                continue
            for dc in range(2):
                pt, pb = ps_next()
                for kc in range(DC):
                    mm(pt[:, :], wv[:, kc, dc * 128:(dc + 1) * 128], rhs.ap[:, kc, :], kc == 0, kc == DC - 1,
                       [sl.bufs[0], rhs.bufs[kc]], [pb])
                evac(nb * 2 + dc, pt, pb)
                yield

    def proj_feature_major(w_d, blk0, nblk, rhs, evac, interleave_first=False):
        for _ in proj_feature_major_gen(w_d, blk0, nblk, rhs, evac, interleave_first):
            pass

    def accum_into_x(w_d, rhs):
        acc = StatAcc(xT)

        def ev(ch, pt, pb):
            tten(xT.ap[:, ch, :], pt[:, :], xT.ap[:, ch, :], ALU.add, [pb, xT.bufs[ch]], [xT.bufs[ch]])
            acc.push(ch)
        for i, _ in enumerate(proj_feature_major_gen(w_d, 0, 8, rhs, ev)):
            if i % 2 == 1 and len(acc.pending) > 2:
                keep = acc.pending[-2:]
                acc.pending = acc.pending[:-2]
                acc.flush()
                acc.pending = keep
        acc.finish()

    def proj_token_major(w_d, blk0, nblk, evac, interleave_first=False):
        for nb in range(nblk):
            sl = wget(w_d[blk0 + nb], SLOT)
            wv = wview(sl, 16)
            if interleave_first and nb == 0:
                banks = [ps_next() for _ in range(4)]
                for kc in range(DC):
                    for tb in range(4):
                        mm(banks[tb][0][:, :CB], hT.ap[:, kc, tb * 128:(tb + 1) * 128], wv[:, kc, :], kc == 0, kc == DC - 1,
                           [sl.bufs[0], hT.bufs[kc]], [banks[tb][1]])
                for tb in range(4):
                    evac(nb, tb, banks[tb][0], banks[tb][1])
                continue
            for tb in range(4):
                pt, pb = ps_next()
                for kc in range(DC):
                    mm(pt[:, :CB], hT.ap[:, kc, tb * 128:(tb + 1) * 128], wv[:, kc, :], kc == 0, kc == DC - 1,
                       [sl.bufs[0], hT.bufs[kc]], [pb])
                evac(nb, tb, pt, pb)

    def gla_gates(mx):
        a_aug, l_t, expD, gam = mx["a_aug"], mx["l"], mx["expD"], mx["gam"]
        pt, pb = ps_next()
        for kc in range(DC):
            mm(pt[:, :], wa_bf.ap[:, kc, :], hT.ap[:, kc, :], kc == 0, kc == DC - 1,
               [wa_bf.bufs[0], hT.bufs[kc]], [pb])
        vmemset(a_aug.ap[0:32, :], 1.0, a_aug.bufs)
        actf(a_aug.ap[0:16, :], pt[0:16, :], AF.Copy, [pb], a_aug.bufs)
        pg, pgb = ps_next(pin=True)
        for tb in range(4):
            pz, pzb = ps_next()
            mm(pz[:, :], a_aug.ap[0:17, tb * 128:(tb + 1) * 128], cap[0:17, C_WA2:C_WA2 + 512], True, True,
               [a_aug.bufs[0], cst.bufs[0]], [pzb])
            actf(l_t.ap, pz[:, :], AF.Exp, [pzb], l_t.bufs, scale=-1.0)
            actf(l_t.ap, l_t.ap, AF.Ln, l_t.bufs, l_t.bufs, bias=1.0)
            pd, pdb = ps_next()
            mm(pd[:, :], cap[:, C_MTRI:C_MTRI + 128], l_t.ap, True, True, [cst.bufs[0], l_t.bufs[0]], [pdb])
            actf(expD.ap[:, tb, :], pd[:, :], AF.Exp, [pdb], [expD.bufs[tb]])
            for h in range(4):
                col = (tb * 4 + h) * 2
                mm(pg[:, col:col + 2], l_t.ap[:, h * 128:(h + 1) * 128], cap[:, C_IND:C_IND + 2], True, True,
                   [l_t.bufs[0], cst.bufs[0]], [pgb], signal=True)
        actf(gam.ap, pg[:, 0:32], AF.Exp, [pgb], gam.bufs)
        ps_unpin(pgb)

    def gla_kv_proj(mx, with_g, parts=("k", "v", "g")):
        expD, kdec, v = mx["expD"], mx["kdec"], mx["v"]

        def ev_k(nb, tb, pt, pb):
            tten(kdec.ap[:, tb, nb * CB:(nb + 1) * CB], pt[:, :CB], expD.ap[:, tb, nb * CB:(nb + 1) * CB], ALU.mult,
                 [pb, expD.bufs[tb]], [kdec.bufs[tb]])
        if "k" in parts:
            proj_token_major(win_d, 6, 2, ev_k)

        def ev_v(nb, tb, pt, pb):
            actf(v.ap[:, tb, nb * CB:(nb + 1) * CB], pt[:, :CB], AF.Copy, [pb], [v.bufs[tb]])
        if "v" in parts:
            proj_token_major(win_d, 8, 4, ev_v, interleave_first=True)
        if with_g and "g" in parts:
            G2, gt = mx["G2"], mx["gtmp"]

            def ev_g(nb, tb, pt, pb):
                g = gt[(nb * 4 + tb) % 2]
                actf(g.ap, pt[:, :CB], AF.Silu, [pb], g.bufs)
                tten(G2.ap[:, tb, nb * CB:(nb + 1) * CB], g.ap, cap[:, C_HNORM + nb * CB:C_HNORM + (nb + 1) * CB], ALU.mult,
                     [g.bufs[0], cst.bufs[0]], [G2.bufs[tb]])
            proj_token_major(win_d, 12, 4, ev_g)

    def gla_recur(mx, S2, tile_idx, full, filler=None):
        kdec, v, gam = mx["kdec"], mx["v"], mx["gam"]
        gam4 = gam.ap.rearrange("p (t h c) -> p t h c", h=4, c=2)

        def step():
            if filler is not None:
                next(filler, None)

        def stage_a(c):
            tb, half = c // 2, c % 2
            rows = slice(half * 64, half * 64 + 64)
            Sin, Sout = S2[c % 2], S2[(c + 1) % 2]
            banks = [ps_next(), ps_next()]
            for h in range(4):
                pt, pb = banks[h // 2]
                mm(pt[:, (h % 2) * 256:(h % 2) * 256 + 256], kdec.ap[rows, tb, h * 128:(h + 1) * 128],
                   v.ap[rows, tb, h * 256:(h + 1) * 256], True, True, [kdec.bufs[tb], v.bufs[tb]], [pb], signal=True)
            for h in range(4):
                pt, pb = banks[h // 2]
                stt(Sout.ap[:, h, :], Sin.ap[:, h, :], gam4[:, tb, h, half:half + 1], pt[:, (h % 2) * 256:(h % 2) * 256 + 256],
                    ALU.mult, ALU.add, [Sin.bufs[h], gam.bufs[0], pb], [Sout.bufs[h]])
            if full:
                Sbf = mx["Sbf"][c % 2]
                for h in range(4):
                    actf(Sbf.ap[:, h, :], Sout.ap[:, h, :], AF.Copy, [Sout.bufs[h]], [Sbf.bufs[h]])

        def stage_b(c):
            tb, half = c // 2, c % 2
            rows = slice(half * 64, half * 64 + 64)
            Sbf = mx["Sbf"][c % 2]
            qT, G2, y, ss, rs, junk, mixT = mx["qT"], mx["G2"], mx["y"], mx["ss"], mx["rs"], mx["junk"], mx["mixT"]
            yb, ssb, rsb, jb = [y.bufs[half]], [ss.bufs[half]], [rs.bufs[half]], [junk.bufs[half]]
            obanks = [ps_next(), ps_next()]
            for h in range(4):
                pt, pb = obanks[h // 2]
                mm(pt[:, (h % 2) * 256:(h % 2) * 256 + 256], qT.ap[:, h, tb * 128:(tb + 1) * 128], Sbf.ap[:, h, :],
                   True, True, [qT.bufs[0], Sbf.bufs[h]], [pb], signal=True)
            for h in range(4):
                pt, pb = obanks[h // 2]
                actf(junk.ap[rows, :], pt[rows, (h % 2) * 256:(h % 2) * 256 + 256], AF.Square, [pb], jb + ssb,
                     accum=ss.ap[rows, h:h + 1])
            actf(rs.ap[rows, :], ss.ap[rows, :], AF.Sqrt, ssb, rsb, scale=1.0 / 256, bias=EPS)
            dve.op(lambda e, o=rs.ap[rows, :]: e.reciprocal(out=o, in_=o), rsb, rsb)
            for h in range(4):
                pt, pb = obanks[h // 2]
                stt(y.ap[rows, h * 256:(h + 1) * 256], pt[rows, (h % 2) * 256:(h % 2) * 256 + 256], rs.ap[rows, h:h + 1],
                    G2.ap[rows, tb, h * 256:(h + 1) * 256], ALU.mult, ALU.mult, [pb, rsb[0], G2.bufs[tb]], yb)

        def stage_c(c):
            tb, half = c // 2, c % 2
            rows = slice(half * 64, half * 64 + 64)
            y, mixT = mx["y"], mx["mixT"]
            yb = [y.bufs[half]]
            ptt, ptb = ps_next()
            for fcx in range(8):
                tr(ptt[:, fcx * 64:(fcx + 1) * 64], y.ap[rows, fcx * 128:(fcx + 1) * 128], cap[rows, C_IDENT + half * 64:C_IDENT + half * 64 + 64],
                   [yb[0], cst.bufs[0]], [ptb], signal=(fcx == 7))
            actf(mixT.ap[:, 8:16, c * 64:(c + 1) * 64], ptt[:, :].rearrange("p (a b) -> p a b", b=64), AF.Copy,
                 [ptb], mixT.bufs[8:16])

        if not full:
            for c in range(8):
                stage_a(c)
                step()
            return
        stage_a(0)
        stage_a(1)
        stage_b(0)
        for c in range(8):
            if c + 2 < 8:
                stage_a(c + 2)
            step()
            if c + 1 < 8:
                stage_b(c + 1)
            step()
            stage_c(c)

    def pool_mixer(mx, tile_idx):
        u_ext, sA, sB, dd, mixT = mx["u_ext"], mx["sA"], mx["sB"], mx["d"], mx["mixT"]
        state = {}

        def ev_u(cc, pt, pb):
            g = cc // 2
            w = 2 ** (g + 1)
            ue = u_ext
            actf(ue.ap[:, 0:16], halo.ap[:, cc, :], AF.Copy, halo.bufs, ue.bufs)
            actf(ue.ap[:, 16:528], pt[:, :], AF.Copy, [pb], ue.bufs)
            actf(halo.ap[:, cc, :], ue.ap[:, 512:528], AF.Copy, ue.bufs, halo.bufs)
            cur, lo = ue, 0
            nxt = [sA, sB]
            for lev in range(g + 1):
                sh = 2 ** lev
                o = nxt[lev % 2]
                tten(o.ap[:, lo + sh:528], cur.ap[:, lo + sh:528], cur.ap[:, lo:528 - sh], ALU.add,
                     cur.bufs, o.bufs)
                cur, lo = o, lo + sh
            dslot = dd[g % 2]
            stt(dslot.ap[:, cc % 2, :], cur.ap[:, 16:528], 1.0 / w, ue.ap[:, 16:528], ALU.mult, ALU.subtract,
                cur.bufs + ue.bufs, dslot.bufs)
            if tile_idx == 0:
                tmp = mx["ptmp"]
                tten(tmp.ap, cur.ap[:, 16:32], cap[:, C_PTAB + g * 16:C_PTAB + (g + 1) * 16], ALU.mult,
                     cur.bufs + cst.bufs, tmp.bufs)
                tten(dslot.ap[:, cc % 2, 0:16], tmp.ap, ue.ap[:, 16:32], ALU.subtract, tmp.bufs + ue.bufs, dslot.bufs)
            if cc % 2 == 1:
                for oc in range(2):
                    pt2, pb2 = ps_next()
                    for kc in range(2):
                        mm(pt2[:, :], poolw.ap[:, g * 2 + kc, oc * 128:(oc + 1) * 128], dslot.ap[:, kc, :], kc == 0, kc == 1,
                           [poolw.bufs[0], dslot.bufs[0]], [pb2])
                    ch = g * 2 + oc
                    actf(mixT.ap[:, ch, :], pt2[:, :], AF.Copy, [pb2, cst.bufs[0]], [mixT.bufs[ch]],
                         scale=cap[:, C_PSCALE + ch:C_PSCALE + ch + 1])
        yield from proj_feature_major_gen(win_d, 0, 4, hT, ev_u)

    def xattn():
        mark = A.mark()
        qx = alloc([128, DC, T], BF16, nbufs=DC)
        p2 = [alloc([128, 4, 256], F32) for _ in range(2)]
        pT = alloc([128, 2, 4, T], BF16)
        oT = alloc([128, DC, T], BF16, nbufs=DC)
        mxt2 = [alloc([128, 4], F32) for _ in range(2)]
        nmx2 = [alloc([128, 4], F32) for _ in range(2)]
        sm2 = [alloc([128, 4], F32) for _ in range(2)]
        rsm2 = [alloc([128, 4], F32) for _ in range(2)]
        norm_to(xT, 2, hT, T, stats=rstd)

        def ev_q(ch, pt, pb):
            actf(qx.ap[:, ch, :], pt[:, :], AF.Copy, [pb], [qx.bufs[ch]], scale=float(512 ** -0.5))
        proj_feature_major(wq_d, 0, 8, hT, ev_q, interleave_first=True)

        lbanks = {}

        def emit_logits(tb):
            banks = [ps_next(), ps_next()]
            lbanks[tb] = banks
            for h in range(4):
                pt, pb = banks[h // 2]
                for dc in range(4):
                    mm(pt[:, (h % 2) * 256:(h % 2) * 256 + 256], qx.ap[:, h * 4 + dc, tb * 128:(tb + 1) * 128],
                       KT.ap[:, h * 4 + dc, :], dc == 0, dc == 3, [qx.bufs[h * 4 + dc], KT.bufs[0]], [pb])

        def emit_softmax(tb):
            banks = lbanks.pop(tb)
            p, mxt, nmx, sm, rsm = p2[tb % 2], mxt2[tb % 2], nmx2[tb % 2], sm2[tb % 2], rsm2[tb % 2]
            for bk in range(2):
                pt, pb = banks[bk]
                dve.op(lambda e, o=mxt.ap[:, bk * 2:bk * 2 + 2], i=pt[:, :].rearrange("p (a b) -> p a b", b=256):
                       e.tensor_reduce(out=o, in_=i, axis=AX.X, op=ALU.max), [pb], mxt.bufs)
            ts(nmx.ap, mxt.ap, -1.0, None, ALU.mult, None, mxt.bufs, nmx.bufs)
            for h in range(4):
                pt, pb = banks[h // 2]
                actf(p.ap[:, h, :], pt[:, (h % 2) * 256:(h % 2) * 256 + 256], AF.Exp, [pb, nmx.bufs[0]], p.bufs,
                     bias=nmx.ap[:, h:h + 1], accum=sm.ap[:, h:h + 1])
                sm.bufs[0].w = dict(p.bufs[0].w)
            dve.op(lambda e, o=rsm.ap, i=sm.ap: e.reciprocal(out=o, in_=i), sm.bufs + p.bufs, rsm.bufs)
            for h in range(4):
                ts(p.ap[:, h, :], p.ap[:, h, :], rsm.ap[:, h:h + 1], None, ALU.mult, None, p.bufs + rsm.bufs, p.bufs)
            tbanks = [ps_next(), ps_next()]
            for mc in range(2):
                pt, pb = tbanks[mc]
                for h in range(4):
                    tr(pt[:, h * 128:(h + 1) * 128], p.ap[:, h, mc * 128:(mc + 1) * 128], ident, [p.bufs[0], cst.bufs[0]], [pb],
                       signal=(h == 3))
                actf(pT.ap[:, mc, :, tb * 128:(tb + 1) * 128], pt[:, :].rearrange("p (a b) -> p a b", b=128), AF.Copy,
                     [pb], pT.bufs)

        emit_logits(0)
        for tb in range(4):
            if tb + 1 < 4:
                emit_logits(tb + 1)
            emit_softmax(tb)
        for ch in range(DC):
            h, dc = ch // 4, ch % 4
            pt, pb = ps_next()
            for mc in range(2):
                mm(pt[:, :], Vt.ap[:, mc, h * 512 + dc * 128:h * 512 + (dc + 1) * 128], pT.ap[:, mc, h, :], mc == 0, mc == 1,
                   [Vt.bufs[0], pT.bufs[0]], [pb])
            actf(oT.ap[:, ch, :], pt[:, :], AF.Copy, [pb], [oT.bufs[ch]])

        accum_into_x(wo_d, oT)
        A.release(mark)

    sp.op(lambda e: e.dma_start(out=cst.ap, in_=cst_d), (), cst.bufs, dma_sem=misc_sem)
    vmemset(ones_bf.ap, 1.0, ones_bf.bufs)
    vmemset(wa_bf.ap.rearrange("p a b -> p (a b)"), 0.0, wa_bf.bufs)
    pool.op(lambda e: e.dma_start(out=wa_bf.ap[:, :, 0:16], in_=wina_d.rearrange("p (a b) -> p a b", b=16)), (), wa_bf.bufs, dma_sem=cc_sem)

    def kv_norm(mhT):
        xm = TT(xT.ap[:, :, 0:256], xT.bufs)
        sp_dma(xm.ap, memT_d, (), xT.bufs)
        norm_to(xm, 5, mhT, 256)

    def k_gen(mhT):
        for nb in range(8):
            sl = wget(wkv_d[nb], SLOT)
            wv = wview(sl, 16)
            for dc in range(2):
                pt, pb = ps_next()
                for kc in range(DC):
                    mm(pt[:, :256], wv[:, kc, dc * 128:(dc + 1) * 128], mhT.ap[:, kc, :], kc == 0, kc == DC - 1,
                       [sl.bufs[0], mhT.bufs[kc]], [pb])
                actf(KT.ap[:, nb * 2 + dc, :], pt[:, :256], AF.Copy, [pb], KT.bufs)
                yield

    def v_gen(mhT):
        for nb in range(8):
            sl = wget(wkv_d[8 + nb], SLOT)
            wv = wview(sl, 16)
            for mc in range(2):
                pt, pb = ps_next()
                for kc in range(DC):
                    mm(pt[:, :CB], mhT.ap[:, kc, mc * 128:(mc + 1) * 128], wv[:, kc, :], kc == 0, kc == DC - 1,
                       [sl.bufs[0], mhT.bufs[kc]], [pb])
                actf(Vt.ap[:, mc, nb * CB:(nb + 1) * CB], pt[:, :CB], AF.Copy, [pb], Vt.bufs)
                yield

    vmemset(Sst.ap, 0.0, Sst.bufs)
    vmemset(halo.ap, 0.0, halo.bufs)
    NPRE = 3 * NT
    xn_mark = A.mark()
    xnext = alloc([128, DC, T], F32, nbufs=DC)
    sp_dma(xnext.ap, xT_d[:, :, 0:T], (), xnext.bufs)
    have_n = False
    mhT = None
    for t in range(NPRE):
        norm_to(xnext, 0, hT, T, stats=rstd if have_n else None)
        ffn(0, xsrc=xnext)
        sp_dma(xnext.ap, xT_d[:, :, (t + 1) * T:(t + 2) * T], (), xnext.bufs)
        norm_to(xT, 1, hT, T, stats=rstd)
        filler = None
        if t == NPRE - 2:
            mh_mark = A.mark()
            mhT = alloc([128, DC, 256], BF16, nbufs=DC)
            activate_top(KT, TOP_KT, 2048)
            filler = k_gen(mhT)
        elif t == NPRE - 1:
            activate_top(Vt, TOP_VT, 2048)
            activate_top(poolw, TOP_PW, 1024)
            pool.op(lambda e: e.dma_start(out=poolw.ap.rearrange("p a b -> p (a b)"), in_=poolw_d), (), poolw.bufs, dma_sem=cc_sem)
            filler = v_gen(mhT)
        mark = A.mark()
        mx = {
            "a_aug": alloc([32, T], F32), "l": alloc([128, 512], F32),
            "expD": alloc([128, 4, 512], F32, nbufs=4), "gam": alloc([128, 32], F32),
            "kdec": alloc([128, 4, 512], BF16, nbufs=4), "v": alloc([128, 4, 1024], BF16, nbufs=4),
        }
        gla_kv_proj(mx, with_g=False, parts=("v",))
        gla_gates(mx)
        gla_kv_proj(mx, with_g=False, parts=("k",))
        if t == NPRE - 2:
            kv_norm(mhT)
        gla_recur(mx, [Sst, Sst2], t, full=False, filler=filler)
        if filler is not None:
            for _ in filler:
                pass
        if t == NPRE - 1:
            for nb in range(4):
                sl = wget(win_d[nb], SLOT)
                wv = wview(sl, 16)
                for dc in range(2):
                    pt, pb = ps_next()
                    for kc in range(DC):
                        mm(pt[:, 0:16], wv[:, kc, dc * 128:(dc + 1) * 128], hT.ap[:, kc, T - 16:T], kc == 0, kc == DC - 1,
                           [sl.bufs[0], hT.bufs[kc]], [pb])
                    cc = nb * 2 + dc
                    actf(halo.ap[:, cc, :], pt[:, 0:16], AF.Copy, [pb], halo.bufs)
        A.release(mark)
        if t == NPRE - 1:
            A.release(mh_mark)
        accn = StatAcc(xnext, T, rstd)
        for c in range(DC):
            accn.push(c)
        accn.finish()
        have_n = True

    rstd_nx = TT(Sst2.ap.rearrange("p h v -> p (h v)")[:, 0:T], Sst2.bufs[0:2])
    for t in range(NT):
        tsl = slice(t * T, (t + 1) * T)
        norm_to(xnext, 0, hT, T, stats=rstd if t == 0 else rstd_nx)
        ffn(0, xsrc=xnext)
        A.release(xn_mark)
        if DEBUG_STAGE == "x1":
            sp_dma(outT_d[:, :, tsl], xT.ap, xT.bufs, [Buf()])
            continue
        norm_to(xT, 1, hT, T, stats=rstd)
        mark = A.mark()
        mx = {
            "a_aug": alloc([32, T], F32), "l": alloc([128, 512], F32),
            "expD": alloc([128, 4, 512], F32, nbufs=4), "gam": alloc([128, 32], F32),
            "kdec": alloc([128, 4, 512], BF16, nbufs=4), "v": alloc([128, 4, 1024], BF16, nbufs=4),
            "G2": alloc([128, 4, 1024], BF16, nbufs=4), "gtmp": [alloc([128, CB], F32) for _ in range(2)],
            "mixT": alloc([128, DC, T], BF16, nbufs=DC), "qT": alloc([128, 4, T], BF16),
            "u_ext": alloc([128, 528], F32), "sA": alloc([128, 528], F32), "sB": alloc([128, 528], F32),
            "d": [alloc([128, 2, T], BF16) for _ in range(2)], "ptmp": alloc([128, 16], F32),
            "y": halves(alloc([128, 1024], F32)), "Sbf": [alloc([128, 4, 256], BF16, nbufs=4) for _ in range(2)],
            "junk": halves(alloc([128, 256], BF16)), "ss": halves(alloc([128, 4], F32)), "rs": halves(alloc([128, 4], F32)),
        }
        gla_kv_proj(mx, with_g=True, parts=("v",))
        gla_gates(mx)
        qT = mx["qT"]

        def ev_qg(ch, pt, pb, qT=qT):
            actf(qT.ap[:, ch, :], pt[:, :], AF.Copy, [pb], qT.bufs, scale=float(128 ** -0.5))
        proj_feature_major(win_d, 4, 2, hT, ev_qg)
        gla_kv_proj(mx, with_g=True, parts=("k", "g"))
        filler = pool_mixer(mx, t)
        gla_recur(mx, [Sst, Sst2], t, full=True, filler=filler)
        for _ in filler:
            pass
        mixT = mx["mixT"]

        accum_into_x(wout_d, mixT)
        A.release(mark)
        if DEBUG_STAGE == "x2":
            sp_dma(outT_d[:, :, tsl], xT.ap, xT.bufs, [Buf()])
            continue
        xattn()
        if DEBUG_STAGE == "x3":
            sp_dma(outT_d[:, :, tsl], xT.ap, xT.bufs, [Buf()])
            continue
        if t + 1 < NT:
            xn_mark = A.mark()
            xnext = alloc([128, DC, T], F32, nbufs=DC)
            sp_dma(xnext.ap, xT_d[:, :, (NPRE + t + 1) * T:(NPRE + t + 2) * T], (), xnext.bufs)
        norm_to(xT, 3, hT, T, stats=rstd)
        hook = None
        if t + 1 < NT:
            def hook(xn=xnext):
                accn2 = StatAcc(xn, T, rstd_nx)
                for c in range(DC):
                    accn2.push(c)
                accn2.finish()
        ffn(1, mid_hook=hook)
        mark = A.mark()
        outb = alloc([128, DC, T], F32, nbufs=DC)
        for c in range(DC):
            stt(outb.ap[:, c, :], xT.ap[:, c, :], gain_ap(4, c), rstd.ap[:, :T], ALU.mult, ALU.mult,
                [xT.bufs[c], rstd.bufs[0], cst.bufs[0]], [outb.bufs[c]])
            if c % 4 == 3:
                sp_dma(outT_d[:, c - 3:c + 1, tsl], outb.ap[:, c - 3:c + 1, :], outb.bufs[c - 3:c + 1], [Buf()])
        A.release(mark)

    for s in sp_sems:
        if s.n:
            sp._wait(s, s.n)

    block = es.enter_context(nc.Block())

    @block.tensor
    def _(e):
        pe.replay(e)

    @block.scalar
    def _(e):
        act.replay(e)

    @block.vector
    def _(e):
        dve.replay(e)

    @block.gpsimd
    def _(e):
        pool.replay(e)

    @block.sync
    def _(e):
        sp.replay(e)

    es.close()
    return nc


def _tile_w(W, cb, kgroups=None):
    K, N = W.shape
    KC = K // 128
    if kgroups is None:
        kgroups = [KC]
    Wr = W.reshape(KC, 128, N // cb, cb)
    blocks = []
    for nb in range(N // cb):
        k0 = 0
        for kn in kgroups:
            blk = Wr[k0:k0 + kn, :, nb, :]
            blocks.append(np.ascontiguousarray(blk.transpose(1, 0, 2)).reshape(128, kn * cb))
            k0 += kn
    return np.ascontiguousarray(np.stack(blocks))


def _fm(v, nch):
    return np.ascontiguousarray(np.asarray(v, np.float32).reshape(nch, 128).T)


_PROGRAM = None


def kernel(x, mem, ffn1_norm, ffn1_w_gate, ffn1_w_up, ffn1_w_down, mix_norm, w_in,
           pool_w, pool_scale, gla_w_a2, gla_b_a, gla_head_norm, w_out,
           xattn_norm, mem_norm, xattn_w_q, xattn_w_kv, xattn_w_o,
           ffn2_norm, ffn2_w_gate, ffn2_w_up, ffn2_w_down, final_norm):
    global _PROGRAM
    f = lambda a: np.asarray(a, np.float32)
    x, mem = f(x), f(mem)
    shared = {
        "wg1": _tile_w(f(ffn1_w_gate)[0], CB), "wu1": _tile_w(f(ffn1_w_up)[0], CB),
        "wd1": _tile_w(f(ffn1_w_down)[0], CB, [11] * 4),
        "wg2": _tile_w(f(ffn2_w_gate)[0], CB), "wu2": _tile_w(f(ffn2_w_up)[0], CB),
        "wd2": _tile_w(f(ffn2_w_down)[0], CB, [11] * 4),
        "win": _tile_w(f(w_in)[0][:, :4096], CB),
        "wina": np.ascontiguousarray(f(w_in)[0][:, 4096:4112].reshape(16, 128, 16).transpose(1, 0, 2)).reshape(128, 256),
        "poolw": np.ascontiguousarray(f(pool_w)[0].reshape(4, 2, 128, 256).transpose(2, 0, 1, 3)).reshape(128, 2048),
        "wout": _tile_w(f(w_out)[0], CB), "wq": _tile_w(f(xattn_w_q)[0], CB),
        "wkv": _tile_w(f(xattn_w_kv)[0], CB), "wo": _tile_w(f(xattn_w_o)[0], CB),
    }
    cst = np.zeros((128, NCST), np.float32)
    gains = [ffn1_norm[0], mix_norm[0], xattn_norm[0], ffn2_norm[0], final_norm, mem_norm[0]]
    for i, g in enumerate(gains):
        cst[:, C_GAIN + i * 16:C_GAIN + (i + 1) * 16] = _fm(g, 16)
    cst[:, C_PSCALE:C_PSCALE + 8] = _fm(f(pool_scale)[0], 8)
    cst[:, C_HNORM:C_HNORM + 1024] = f(gla_head_norm)[0][None, :]
    cst[0:16, C_WA2:C_WA2 + 512] = f(gla_w_a2)[0]
    cst[16, C_WA2:C_WA2 + 512] = f(gla_b_a)[0]
    ii = np.arange(128)
    same = (ii[:, None] // 64) == (ii[None, :] // 64)
    cst[:, C_MTRI:C_MTRI + 128] = np.where(same & (ii[:, None] > ii[None, :]), -1.0 / 16.0, 0.0)
    cst[:, C_IND:C_IND + 2] = np.where((ii[:, None] // 64) == np.arange(2)[None, :], -1.0 / 16.0, 0.0)
    cst[:, C_IDENT:C_IDENT + 128] = np.eye(128, dtype=np.float32)

    in_maps = []
    for core in range(NCORES):
        b, q = core // 4, core % 4
        xs = np.zeros((4 * TOK, D), np.float32)
        xs[(3 - q) * TOK:] = x[b, 0:(q + 1) * TOK, :]
        xTc = np.ascontiguousarray(xs.reshape(4 * TOK, DC, 128).transpose(2, 1, 0))
        memTc = np.ascontiguousarray(mem[b].reshape(256, DC, 128).transpose(2, 1, 0))
        c = cst.copy()
        for r in range(NCORES):
            m = 1.0 if (r // 4 == b and r % 4 < q) else 0.0
            c[:, C_M + r] = m
            c[:, C_ONEM + r] = 1.0 - m
            c[:, C_HM + r] = 1.0 if (r == core - 1 and q > 0) else 0.0
        for g in range(4):
            w = 2 ** (g + 1)
            tt = np.arange(16)
            c[:, C_PTAB + g * 16:C_PTAB + (g + 1) * 16] = (1.0 / np.minimum(tt + 1, w) if q == 0 else np.full(16, 1.0 / w))[None, :]
        d = {"xT": xTc, "memT": memTc, "cst": c}
        d.update(shared)
        in_maps.append(d)

    if _PROGRAM is None:
        _PROGRAM = build_program()
    res = run_bass_kernel_spmd(_PROGRAM, in_maps, core_ids=list(range(NCORES)))
    out = np.empty((2, 8192, D), np.float32)
    for core in range(NCORES):
        b, q = core // 4, core % 4
        oT = np.asarray(res.results[core]["outT"])
        out[b, q * TOK:(q + 1) * TOK, :] = oT.transpose(2, 1, 0).reshape(TOK, D)
    return out
```
